# Optimizing a Trainium2 kernel written in Bass

```python
import math
import jax, jax.numpy as jnp
from jax import lax
import numpy as np

D_MODEL = 1024
BATCH = 2
SEQ = 8192
DEPTH = 4

N_MEM = 256
HEAD_DIM = 64
ROPE_THETA = 500000.0
ROPE_DIM = HEAD_DIM // 4
NORM_EPS = 1e-6
NEG_INF = -1e30
ATTN_Q_BLOCK = 128

A_HEADS = 6
MOBA_BLOCK = 256
MOBA_TOPK = 3
MOBA_Q_BLOCK = 64
B_HEADS = 6
B_LORA_W = 32
B_LORA_A = 32
RWKV_GN_EPS = 64e-5
C_HEADS = 4
C_VDIM = 2 * HEAD_DIM
D_HEADS = 4
D_CHUNK = 64
M_HEADS = 4

A_W = A_HEADS * HEAD_DIM
B_W = B_HEADS * HEAD_DIM
C_W = C_HEADS * C_VDIM
D_W = D_HEADS * HEAD_DIM
M_W = M_HEADS * HEAD_DIM
MIX_W_EVEN = A_W + B_W + M_W
MIX_W_ODD = C_W + D_W + M_W
B_SHIFT_W = 3 * B_W + B_LORA_W + B_LORA_A
EVEN_SIZES = (A_W, A_W, A_W, A_W, B_SHIFT_W, B_W, M_W, M_W)
ODD_SIZES = (2 * C_HEADS * HEAD_DIM, 2 * C_HEADS * HEAD_DIM, C_W, C_W, D_W, D_W, D_W, D_W, M_W, M_W)
EVEN_IN = sum(EVEN_SIZES)
ODD_IN = sum(ODD_SIZES)
N_EVEN = (DEPTH + 1) // 2
N_ODD = DEPTH // 2

kernel_name = 'hybrid_moba_rwkv7_diffattn_hgrn2'

F32 = jnp.float32


def rms_norm(x, g, eps=NORM_EPS):
    xf = x.astype(F32)
    y = xf * lax.rsqrt(jnp.mean(xf * xf, axis=-1, keepdims=True) + eps)
    return (y * g.astype(F32)).astype(x.dtype)


def split_last(t, sizes):
    return jnp.split(t, np.cumsum(sizes)[:-1].tolist(), axis=-1)


def split_heads(t, n):
    b, s, w = t.shape
    return t.reshape(b, s, n, w // n).transpose(0, 2, 1, 3)


def merge_heads(t):
    b, h, s, d = t.shape
    return t.transpose(0, 2, 1, 3).reshape(b, s, h * d)


def rope_tables(seq):
    pos = jnp.arange(seq, dtype=F32)
    inv = 1.0 / (ROPE_THETA ** (jnp.arange(0, ROPE_DIM, 2, dtype=F32) / ROPE_DIM))
    ang = pos[:, None] * inv[None, :]
    return jnp.cos(ang), jnp.sin(ang)


def partial_rope(x, cos, sin):
    xr = x[..., :ROPE_DIM].astype(F32)
    x1, x2 = xr[..., :ROPE_DIM // 2], xr[..., ROPE_DIM // 2:]
    rot = jnp.concatenate([x1 * cos - x2 * sin, x1 * sin + x2 * cos], axis=-1).astype(x.dtype)
    return jnp.concatenate([rot, x[..., ROPE_DIM:]], axis=-1)


def token_shift(t, mu):
    prev = jnp.pad(t, ((0, 0), (1, 0), (0, 0)))[:, :-1]
    return t + (prev - t) * mu


def moba_attention(q, k, v):
    bn, h, s, dh = q.shape
    nb = -(-s // MOBA_BLOCK)
    pad = nb * MOBA_BLOCK - s
    padw = ((0, 0), (0, 0), (0, pad), (0, 0))
    kp, vp = jnp.pad(k, padw), jnp.pad(v, padw)
    kb = kp.reshape(bn, h, nb, MOBA_BLOCK, dh)
    vb = vp.reshape(bn, h, nb, MOBA_BLOCK, dh)
    kmean = jnp.mean(kb.astype(F32), axis=3)
    topk = min(MOBA_TOPK, nb)
    scale = dh ** -0.5
    bidx = jnp.arange(bn)[:, None, None, None]
    hidx = jnp.arange(h)[None, :, None, None]
    blk_ids = jnp.arange(nb)
    own_off = jnp.arange(MOBA_BLOCK)

    def block(qi):
        q0 = qi * MOBA_Q_BLOCK
        qb = lax.dynamic_slice_in_dim(q, q0, MOBA_Q_BLOCK, axis=2)
        qpos = q0 + jnp.arange(MOBA_Q_BLOCK)
        own = q0 // MOBA_BLOCK
        gate = jnp.einsum('bhqd,bhnd->bhqn', qb.astype(F32), kmean)
        gate = jnp.where(blk_ids < own, gate, NEG_INF)
        _, sel = lax.top_k(gate, topk)
        valid = (sel < own)[..., None]
        ksel = kb[bidx, hidx, sel]
        vsel = vb[bidx, hidx, sel]
        s_sel = jnp.einsum('bhqd,bhqtld->bhqtl', qb, ksel).astype(F32) * scale
        s_sel = jnp.where(valid, s_sel, NEG_INF).reshape(bn, h, MOBA_Q_BLOCK, topk * MOBA_BLOCK)
        k_own = lax.dynamic_slice_in_dim(kp, own * MOBA_BLOCK, MOBA_BLOCK, axis=2)
        v_own = lax.dynamic_slice_in_dim(vp, own * MOBA_BLOCK, MOBA_BLOCK, axis=2)
        s_own = jnp.einsum('bhqd,bhld->bhql', qb, k_own).astype(F32) * scale
        s_own = jnp.where(own * MOBA_BLOCK + own_off[None, :] <= qpos[:, None], s_own, NEG_INF)
        p = jax.nn.softmax(jnp.concatenate([s_sel, s_own], axis=-1), axis=-1).astype(v.dtype)
        p_sel = p[..., :topk * MOBA_BLOCK].reshape(bn, h, MOBA_Q_BLOCK, topk, MOBA_BLOCK)
        p_own = p[..., topk * MOBA_BLOCK:]
        return (jnp.einsum('bhqtl,bhqtld->bhqd', p_sel, vsel)
                + jnp.einsum('bhql,bhld->bhqd', p_own, v_own))

    out = lax.map(block, jnp.arange(s // MOBA_Q_BLOCK))
    return jnp.moveaxis(out, 0, 2).reshape(bn, h, s, dh)


def rwkv7_scan(r, w, k, v, a, b):
    bn, s, h, dh = r.shape
    xs = tuple(jnp.moveaxis(t, 1, 0) for t in (r, w, k, v, a, b))

    def step(st, inp):
        r_t, w_t, k_t, v_t, a_t, b_t = inp
        sa = jnp.einsum('bhvk,bhk->bhv', st, a_t)
        st = st * w_t[:, :, None, :] + sa[..., None] * b_t[:, :, None, :] + v_t[..., None] * k_t[:, :, None, :]
        return st, jnp.einsum('bhvk,bhk->bhv', st, r_t)

    _, ys = lax.scan(step, jnp.zeros((bn, h, dh, dh), F32), xs)
    return jnp.moveaxis(ys, 0, 1)


def rwkv7_time_mix(bs, mu, w0, w2, a0, a2, k_k, k_a, r_k, lnx_g, lnx_b):
    bn, s, _ = bs.shape
    bs = token_shift(bs, mu)
    r, k, v, wl, al = (t.astype(F32) for t in split_last(bs, (B_W, B_W, B_W, B_LORA_W, B_LORA_A)))
    w_log = -jax.nn.softplus(-(w0.astype(F32) + jnp.tanh(wl) @ w2.astype(F32))) - 0.5
    decay = jnp.exp(-jnp.exp(w_log))
    a = jax.nn.sigmoid(a0.astype(F32) + al @ a2.astype(F32))
    kk = k * k_k.astype(F32)
    k = k * (1.0 + (a - 1.0) * k_a.astype(F32))
    hs = lambda t: t.reshape(bn, s, B_HEADS, HEAD_DIM)
    r, k, v, kk, a, decay = map(hs, (r, k, v, kk, a, decay))
    kk = kk * lax.rsqrt(jnp.maximum(jnp.sum(kk * kk, axis=-1, keepdims=True), 1e-24))
    y = rwkv7_scan(r, decay, k, v, -kk, kk * a)
    mean = jnp.mean(y, axis=-1, keepdims=True)
    var = jnp.mean(jnp.square(y - mean), axis=-1, keepdims=True)
    y = ((y - mean) * lax.rsqrt(var + RWKV_GN_EPS)).reshape(bn, s, B_W) * lnx_g.astype(F32) + lnx_b.astype(F32)
    bonus = jnp.sum(r * k * r_k.astype(F32), axis=-1, keepdims=True) * v
    return (y + bonus.reshape(bn, s, B_W)).astype(bs.dtype)


def diff_attention(q, k, v, lam):
    bn, h, _, s, dh = q.shape
    scale = dh ** -0.5
    kpos = jnp.arange(s)

    def block(qi):
        q0 = qi * ATTN_Q_BLOCK
        qb = lax.dynamic_slice_in_dim(q, q0, ATTN_Q_BLOCK, axis=3)
        qpos = q0 + jnp.arange(ATTN_Q_BLOCK)
        sc = jnp.einsum('bhmqd,bhmkd->bhmqk', qb, k).astype(F32) * scale
        sc = jnp.where(kpos[None, :] <= qpos[:, None], sc, NEG_INF)
        p = jax.nn.softmax(sc, axis=-1)
        pd = p[:, :, 0] - lam * p[:, :, 1]
        return jnp.einsum('bhqk,bhkd->bhqd', pd.astype(v.dtype), v)

    out = lax.map(block, jnp.arange(s // ATTN_Q_BLOCK))
    return jnp.moveaxis(out, 0, 2).reshape(bn, h, s, v.shape[-1])


def hgrn2_chunked(q, k, v, log_f):
    bn, h, s, dk = q.shape
    dv = v.shape[-1]
    nc = s // D_CHUNK
    chunks = lambda t: jnp.moveaxis(t.reshape(bn, h, nc, D_CHUNK, t.shape[-1]), 2, 0)
    causal = jnp.tril(jnp.ones((D_CHUNK, D_CHUNK), dtype=bool))[:, :, None]

    def step(st, inp):
        qt, kt, vt, gt = inp
        bcum = jnp.cumsum(gt, axis=2)
        inter = jnp.einsum('bhtd,bhde->bhte', qt * jnp.exp(bcum), st)
        diff = jnp.where(causal, bcum[:, :, :, None, :] - bcum[:, :, None, :, :], NEG_INF)
        scores = jnp.einsum('bhtd,bhsd,bhtsd->bhts', qt, kt, jnp.exp(diff))
        intra = jnp.einsum('bhts,bhse->bhte', scores, vt)
        b_last = bcum[:, :, -1:, :]
        st = st * jnp.exp(b_last[:, :, 0, :])[..., None] + jnp.einsum('bhsd,bhse->bhde', kt * jnp.exp(b_last - bcum), vt)
        return st, inter + intra

    _, o = lax.scan(step, jnp.zeros((bn, h, dk, dv), F32), tuple(map(chunks, (q, k, v, log_f))))
    return jnp.moveaxis(o, 0, 2).reshape(bn, h, s, dv)


def memory_kv(mem, g, w_kv, kn_g):
    km, vm = split_last(rms_norm(mem, g) @ w_kv, (M_W, M_W))
    return rms_norm(split_heads(km, M_HEADS), kn_g), split_heads(vm, M_HEADS)


def memory_attention(q, km, vm):
    sc = jnp.einsum('bhsd,bhmd->bhsm', q, km).astype(F32) * HEAD_DIM ** -0.5
    p = jax.nn.softmax(sc, axis=-1).astype(vm.dtype)
    return merge_heads(jnp.einsum('bhsm,bhmd->bhsd', p, vm))


def even_layer(h, km, vm, cos, sin, w_in, w_out, a_qn_g, a_kn_g, m_qn_g,
               b_mu, b_w0, b_w2, b_a0, b_a2, b_k_k, b_k_a, b_r_k, b_lnx_g, b_lnx_b):
    qa, ka, va, ga, bs, gb, qm, gm = split_last(h @ w_in, EVEN_SIZES)
    qa = partial_rope(rms_norm(split_heads(qa, A_HEADS), a_qn_g), cos, sin)
    ka = partial_rope(rms_norm(split_heads(ka, A_HEADS), a_kn_g), cos, sin)
    oa = merge_heads(moba_attention(qa, ka, split_heads(va, A_HEADS)))
    ob = rwkv7_time_mix(bs, b_mu, b_w0, b_w2, b_a0, b_a2, b_k_k, b_k_a, b_r_k, b_lnx_g, b_lnx_b)
    om = memory_attention(rms_norm(split_heads(qm, M_HEADS), m_qn_g), km, vm)
    mixed = jnp.concatenate([oa * jax.nn.silu(ga), ob * jax.nn.silu(gb), om * jax.nn.silu(gm)], axis=-1)
    return mixed @ w_out


def odd_layer(h, km, vm, cos, sin, li, lb, w_in, w_out, c_qn_g, c_kn_g,
              c_lq1, c_lk1, c_lq2, c_lk2, c_subln_g, d_gn_g, m_qn_g):
    bn, s, _ = h.shape
    qc, kc, vc, gc, qd, fd, idd, gd, qm, gm = split_last(h @ w_in, ODD_SIZES)
    pair_heads = lambda t: t.reshape(bn, s, C_HEADS, 2, HEAD_DIM).transpose(0, 2, 3, 1, 4)
    qc = partial_rope(rms_norm(pair_heads(qc), c_qn_g), cos, sin)
    kc = partial_rope(rms_norm(pair_heads(kc), c_kn_g), cos, sin)
    lam_init = 0.8 - 0.6 * math.exp(-0.3 * li)
    lam = (jnp.exp(jnp.sum(c_lq1.astype(F32) * c_lk1.astype(F32)))
           - jnp.exp(jnp.sum(c_lq2.astype(F32) * c_lk2.astype(F32))) + lam_init)
    oc = diff_attention(qc, kc, split_heads(vc, C_HEADS), lam)
    oc = merge_heads(rms_norm(oc, c_subln_g) * (1.0 - lam_init))
    fr = fd.astype(F32)
    log_f = jnp.logaddexp(jnp.log(lb), jnp.log1p(-lb) + jax.nn.log_sigmoid(fr))
    ig = (1.0 - lb) * jax.nn.sigmoid(-fr)
    od = hgrn2_chunked(split_heads(qd.astype(F32), D_HEADS), split_heads(ig, D_HEADS),
                       split_heads(idd.astype(F32), D_HEADS), split_heads(log_f, D_HEADS))
    od = merge_heads(rms_norm(od, d_gn_g)).astype(h.dtype)
    om = memory_attention(rms_norm(split_heads(qm, M_HEADS), m_qn_g), km, vm)
    mixed = jnp.concatenate([oc * jax.nn.silu(gc), od * jax.nn.silu(gd), om * jax.nn.silu(gm)], axis=-1)
    return mixed @ w_out


def setup_inputs(seed: int = 0) -> dict:
    key = jax.random.key(seed)
    ks = iter(jax.random.split(key, 40))
    nrm = lambda shape, scale: scale * jax.random.normal(next(ks), shape, F32)
    gain = lambda shape: 1.0 + 0.02 * jax.random.normal(next(ks), shape, F32)
    return {
        'x': nrm((BATCH, SEQ, D_MODEL), 1.0),
        'mem': nrm((BATCH, N_MEM, D_MODEL), 1.0),
        'ln_g': gain((DEPTH, D_MODEL)),
        'mem_ln_g': gain((DEPTH, D_MODEL)),
        'w_mem_kv': nrm((DEPTH, D_MODEL, 2 * M_W), D_MODEL ** -0.5),
        'm_qn_g': gain((DEPTH, HEAD_DIM)),
        'm_kn_g': gain((DEPTH, HEAD_DIM)),
        'e_w_in': nrm((N_EVEN, D_MODEL, EVEN_IN), D_MODEL ** -0.5),
        'e_w_out': nrm((N_EVEN, MIX_W_EVEN, D_MODEL), MIX_W_EVEN ** -0.5),
        'a_qn_g': gain((N_EVEN, HEAD_DIM)),
        'a_kn_g': gain((N_EVEN, HEAD_DIM)),
        'b_mu': jax.random.uniform(next(ks), (N_EVEN, B_SHIFT_W), F32),
        'b_w0': -2.0 + nrm((N_EVEN, B_W), 1.0),
        'b_w2': nrm((N_EVEN, B_LORA_W, B_W), 0.5 * B_LORA_W ** -0.5),
        'b_a0': nrm((N_EVEN, B_W), 0.1),
        'b_a2': nrm((N_EVEN, B_LORA_A, B_W), 0.5 * B_LORA_A ** -0.5),
        'b_k_k': 0.85 + nrm((N_EVEN, B_W), 0.05),
        'b_k_a': 1.0 + nrm((N_EVEN, B_W), 0.05),
        'b_r_k': nrm((N_EVEN, B_HEADS, HEAD_DIM), 0.1),
        'b_lnx_g': gain((N_EVEN, B_W)),
        'b_lnx_b': nrm((N_EVEN, B_W), 0.02),
        'o_w_in': nrm((N_ODD, D_MODEL, ODD_IN), D_MODEL ** -0.5),
        'o_w_out': nrm((N_ODD, MIX_W_ODD, D_MODEL), MIX_W_ODD ** -0.5),
        'c_qn_g': gain((N_ODD, HEAD_DIM)),
        'c_kn_g': gain((N_ODD, HEAD_DIM)),
        'c_lq1': nrm((N_ODD, HEAD_DIM), 0.1),
        'c_lk1': nrm((N_ODD, HEAD_DIM), 0.1),
        'c_lq2': nrm((N_ODD, HEAD_DIM), 0.1),
        'c_lk2': nrm((N_ODD, HEAD_DIM), 0.1),
        'c_subln_g': gain((N_ODD, C_VDIM)),
        'd_lb': nrm((N_ODD, D_W), 0.1),
        'd_gn_g': gain((N_ODD, HEAD_DIM)),
    }


def reference(x, mem, ln_g, mem_ln_g, w_mem_kv, m_qn_g, m_kn_g, e_w_in, e_w_out, a_qn_g, a_kn_g,
              b_mu, b_w0, b_w2, b_a0, b_a2, b_k_k, b_k_a, b_r_k, b_lnx_g, b_lnx_b,
              o_w_in, o_w_out, c_qn_g, c_kn_g, c_lq1, c_lk1, c_lq2, c_lk2, c_subln_g, d_lb, d_gn_g):
    cos, sin = rope_tables(x.shape[1])
    lbs = jax.nn.softmax(d_lb.astype(F32), axis=0)
    lbs = jnp.cumsum(lbs, axis=0) - lbs[0:1]
    for li in range(DEPTH):
        j = li // 2
        h = rms_norm(x, ln_g[li])
        km, vm = memory_kv(mem, mem_ln_g[li], w_mem_kv[li], m_kn_g[li])
        if li % 2 == 0:
            y = even_layer(h, km, vm, cos, sin, e_w_in[j], e_w_out[j], a_qn_g[j], a_kn_g[j], m_qn_g[li],
                           b_mu[j], b_w0[j], b_w2[j], b_a0[j], b_a2[j], b_k_k[j], b_k_a[j], b_r_k[j],
                           b_lnx_g[j], b_lnx_b[j])
        else:
            lb = jnp.maximum(lbs[j], 0.0)
            y = odd_layer(h, km, vm, cos, sin, li, lb, o_w_in[j], o_w_out[j], c_qn_g[j], c_kn_g[j],
                          c_lq1[j], c_lk1[j], c_lq2[j], c_lk2[j], c_subln_g[j], d_gn_g[j], m_qn_g[li])
        x = x + y
    return x
```

```python
import numpy as np
from contextlib import ExitStack
import concourse.bass as bass
import concourse.mybir as mybir
from concourse.bass_utils import run_bass_kernel_spmd

F32 = mybir.dt.float32
BF16 = mybir.dt.bfloat16
AF = mybir.ActivationFunctionType
ALU = mybir.AluOpType
AX = mybir.AxisListType

NCORES = 8
D = 1024
B = 2
S = 8192
DEPTH = 4
NMEM = 256
HD = 64
EPS = 1e-6


class Buf:
    def __init__(self, ap, name=None):
        self.ap = ap
        self.name = name
        self.w = None
        self.r = {}

    def __getitem__(self, k):
        return self.ap[k]


class Prog:
    NDMA = 8

    def __init__(self, nc, es):
        self.nc = nc
        self.es = es
        self.thunks = {k: [] for k in ['pe', 'act', 'dve', 'pool', 'sp']}
        self.sem = {}
        self.cnt = {}
        for k in ['pe', 'act', 'dve', 'pool']:
            self.sem[k] = es.enter_context(nc.semaphore('s_' + k))
            self.cnt[k] = 0
        for q in ['sp', 'pool']:
            for i in range(self.NDMA):
                key = ('dma', q, i)
                self.sem[key] = es.enter_context(nc.semaphore('sd_%s%d' % (q, i)))
                self.cnt[key] = 0
        self.dma_rr = {'sp': 0, 'pool': 0}
        self.waited = {}
        self.ninstr = 0
        self._nm = 0

    def sbuf(self, shape, dtype=F32, name=None):
        self._nm += 1
        name = name or ('sb%d' % self._nm)
        t = self.es.enter_context(self.nc.sbuf_tensor(name, list(shape), dtype))
        return Buf(t, name)

    def psum(self, shape, dtype=F32, name=None):
        self._nm += 1
        name = name or ('ps%d' % self._nm)
        t = self.es.enter_context(self.nc.psum_tensor(name, list(shape), dtype))
        return Buf(t, name)

    def _deps(self, reads, writes):
        deps = {}

        def add(e, n):
            if deps.get(e, 0) < n:
                deps[e] = n
        for b in reads:
            if b.w is not None:
                add(*b.w)
        for b in writes:
            if b.w is not None:
                add(*b.w)
            for e, n in b.r.items():
                add(e, n)
        return deps

    def _emit_waits(self, engname, deps, skip=None):
        for e, n in deps.items():
            if e == skip:
                continue
            if self.waited.get((engname, e), 0) >= n:
                continue
            self.waited[(engname, e)] = n
            sem = self.sem[e]
            self.thunks[engname].append(lambda eng, sem=sem, n=n: eng.wait_ge(sem, n))

    serial = False

    def op(self, engname, fn, reads=(), writes=()):
        deps = self._deps(reads, writes)
        if self.serial:
            for k in ['pe', 'act', 'dve', 'pool']:
                if self.cnt[k] > 0:
                    deps[k] = max(deps.get(k, 0), self.cnt[k])
        self._emit_waits(engname, deps, skip=('pe' if engname == 'pe' else None))
        self.cnt[engname] += 1
        n = self.cnt[engname]
        sem = self.sem[engname]
        self.thunks[engname].append(lambda eng, fn=fn, sem=sem: fn(eng).then_inc(sem, 1))
        for b in reads:
            b.r[engname] = n
        for b in writes:
            b.w = (engname, n)
            b.r = {}
        self.ninstr += 1

    def dma(self, out_ap, in_ap, reads=(), writes=(), q='sp'):
        i = self.dma_rr[q]
        self.dma_rr[q] = (i + 1) % self.NDMA
        key = ('dma', q, i)
        deps = self._deps(reads, writes)
        if self.cnt[key] > 0:
            deps[key] = max(deps.get(key, 0), self.cnt[key])
        self._emit_waits(q, deps)
        self.cnt[key] += 16
        n = self.cnt[key]
        sem = self.sem[key]
        self.thunks[q].append(
            lambda eng, o=out_ap, s=in_ap, sem=sem: eng.dma_start(out=o, in_=s).then_inc(sem, 16))
        for b in reads:
            b.r[key] = n
        for b in writes:
            b.w = (key, n)
            b.r = {}
        self.ninstr += 1

    def barrier(self):
        cur = {k: self.cnt[k] for k in ['pe', 'act', 'dve', 'pool']}
        for eng in ['pe', 'act', 'dve', 'pool']:
            self._emit_waits(eng, {k: n for k, n in cur.items() if k != eng and n > 0})

    def finish(self):
        fin = {key: n for key, n in self.cnt.items() if n > 0}
        self._emit_waits('sp', fin)
        th = self.thunks
        with self.nc.Block() as block:
            @block.sync
            def _(e):
                for t in th['sp']:
                    t(e)

            @block.tensor
            def _(e):
                for t in th['pe']:
                    t(e)

            @block.scalar
            def _(e):
                for t in th['act']:
                    t(e)

            @block.vector
            def _(e):
                for t in th['dve']:
                    t(e)

            @block.gpsimd
            def _(e):
                for t in th['pool']:
                    t(e)


def _rsqrt_mean(P, dst, src, n, eps=EPS):
    P.op('dve', lambda e: e.tensor_scalar(out=dst[:], in0=src[:], scalar1=1.0 / n, scalar2=float(eps),
                                          op0=ALU.mult, op1=ALU.add), reads=[src], writes=[dst])
    P.op('act', lambda e: e.activation(out=dst[:], in_=dst[:], func=AF.Sqrt), reads=[dst], writes=[dst])
    P.op('dve', lambda e: e.reciprocal(out=dst[:], in_=dst[:]), reads=[dst], writes=[dst])


def build_proj(ntok, nin, segs):
    nc = bass.Bass("TRN2", target_bir_lowering=False)
    x = nc.dram_tensor("x", [ntok, D], F32, kind="ExternalInput").ap()
    gx = nc.dram_tensor("gx", [128, D], F32, kind="ExternalInput").ap()
    w = nc.dram_tensor("w", [D, nin], F32, kind="ExternalInput").ap()
    gains = nc.dram_tensor("gains", [128, nin], F32, kind="ExternalInput").ap()
    cosr = nc.dram_tensor("cosr", [ntok, 64], F32, kind="ExternalInput").ap()
    sinr = nc.dram_tensor("sinr", [ntok, 64], F32, kind="ExternalInput").ap()
    ident = nc.dram_tensor("ident", [128, 128], F32, kind="ExternalInput").ap()
    y = nc.dram_tensor("y", [ntok, nin], F32, kind="ExternalOutput").ap()
    nt = ntok // 128
    chunks = [(c, min(512, nin - c)) for c in range(0, nin, 512)]
    with ExitStack() as es:
        P = Prog(nc, es)
        gxt = P.sbuf([128, D]); gnt = P.sbuf([128, nin])
        idf = P.sbuf([128, 128]); idb = P.sbuf([128, 128], BF16)
        wb = P.sbuf([128, 8, nin], BF16)
        wst = [P.sbuf([128, 8, 512]) for _ in range(2)]
        cst = P.sbuf([128, nt, 64]); snt = P.sbuf([128, nt, 64])
        P.dma(gxt[:], gx, writes=[gxt])
        P.dma(gnt[:], gains, writes=[gnt])
        P.dma(idf[:], ident, writes=[idf])
        P.dma(cst[:], cosr.rearrange("(n p) c -> p n c", p=128), writes=[cst])
        P.dma(snt[:], sinr.rearrange("(n p) c -> p n c", p=128), writes=[snt])
        P.op('dve', lambda e: e.tensor_copy(out=idb[:], in_=idf[:]), reads=[idf], writes=[idb])
        wv = w.rearrange("(k p) n -> p k n", p=128)
        for ci, (c0, cw) in enumerate(chunks):
            st = wst[ci % 2]
            P.dma(st[:, :, 0:cw], wv[:, :, c0:c0 + cw], writes=[st], q=('sp' if ci % 2 == 0 else 'pool'))
            eng = 'dve' if ci % 2 == 0 else 'pool'
            P.op(eng, lambda e, st=st, c0=c0, cw=cw: e.tensor_copy(out=wb[:, :, c0:c0 + cw], in_=st[:, :, 0:cw]),
                 reads=[st], writes=[wb])
        xts = [P.sbuf([128, D]) for _ in range(2)]
        sq = P.sbuf([128, D]); ss = P.sbuf([128, 1]); rstd = P.sbuf([128, 1])
        hb = P.sbuf([128, D], BF16)
        hTs = [P.sbuf([128, 8, 128], BF16) for _ in range(2)]
        pT = P.psum([128, 8, 128], BF16)
        pos = [P.psum([128, 512]) for _ in range(3)]
        ots = [P.sbuf([128, nin]) for _ in range(2)]
        sq2 = P.sbuf([128, 512]); ssh = P.sbuf([128, 8]); rsh = P.sbuf([128, 8])
        t1 = P.sbuf([128, 8, 8]); t2 = P.sbuf([128, 8, 8]); t3 = P.sbuf([128, 8, 8]); t4 = P.sbuf([128, 8, 8])
        pi = 0
        for t in range(nt):
            xt = xts[t % 2]; hT = hTs[t % 2]; ot = ots[t % 2]
            P.dma(xt[:], x[t * 128:(t + 1) * 128, :], writes=[xt])
            P.op('act', lambda e, xt=xt: e.activation(out=sq[:], in_=xt[:], func=AF.Square, accum_out=ss[:]),
                 reads=[xt], writes=[sq, ss])
            _rsqrt_mean(P, rstd, ss, D)
            P.op('dve', lambda e, xt=xt: e.scalar_tensor_tensor(out=hb[:], in0=xt[:], scalar=rstd[:, 0:1], in1=gxt[:],
                                                                op0=ALU.mult, op1=ALU.mult),
                 reads=[xt, rstd, gxt], writes=[hb])
            for k in range(8):
                P.op('pe', lambda e, k=k: e.transpose(out=pT[:, k, :], in_=hb[:, k * 128:(k + 1) * 128], identity=idb[:]),
                     reads=[hb, idb], writes=[pT])
            P.op('dve', lambda e, hT=hT: e.tensor_copy(out=hT[:], in_=pT[:]), reads=[pT], writes=[hT])
            for (c0, cw) in chunks:
                po = pos[pi % 3]; pi += 1
                for k in range(8):
                    P.op('pe', lambda e, k=k, po=po, c0=c0, cw=cw, hT=hT: e.matmul(
                        po[:, 0:cw], lhsT=hT[:, k, :], rhs=wb[:, k, c0:c0 + cw], start=(k == 0), stop=(k == 7)),
                        reads=[hT, wb], writes=[po])
                P.op('act', lambda e, po=po, c0=c0, cw=cw, ot=ot: e.copy(out=ot[:, c0:c0 + cw], in_=po[:, 0:cw]),
                     reads=[po], writes=[ot])
            for (kind, s0, nh) in segs:
                wd = nh * 64
                if kind == 'copy':
                    continue
                if kind == 'silu':
                    P.op('act', lambda e, ot=ot, s0=s0, wd=wd: e.activation(out=ot[:, s0:s0 + wd], in_=ot[:, s0:s0 + wd], func=AF.Silu),
                         reads=[ot], writes=[ot])
                    continue
                seg = lambda ot=ot, s0=s0, wd=wd: ot[:, s0:s0 + wd]
                seg3 = lambda ot=ot, s0=s0, wd=wd, nh=nh: ot[:, s0:s0 + wd].rearrange("p (h d) -> p h d", h=nh)
                P.op('act', lambda e, seg=seg, wd=wd: e.activation(out=sq2[:, 0:wd], in_=seg(), func=AF.Square),
                     reads=[ot], writes=[sq2])
                P.op('dve', lambda e, nh=nh, wd=wd: e.tensor_reduce(out=ssh[:, 0:nh], in_=sq2[:, 0:wd].rearrange("p (h d) -> p h d", h=nh),
                                                                  axis=AX.X, op=ALU.add), reads=[sq2], writes=[ssh])
                _rsqrt_mean(P, rsh, ssh, 64)
                P.op('dve', lambda e, seg3=seg3, nh=nh: e.tensor_tensor(out=seg3(), in0=seg3(),
                                                                        in1=rsh[:, 0:nh].unsqueeze(2).to_broadcast([128, nh, 64]), op=ALU.mult),
                     reads=[ot, rsh], writes=[ot])
                P.op('pool', lambda e, seg=seg, s0=s0, wd=wd: e.tensor_tensor(out=seg(), in0=seg(), in1=gnt[:, s0:s0 + wd], op=ALU.mult),
                     reads=[ot, gnt], writes=[ot])
                if kind == 'normrope':
                    x1 = lambda seg3=seg3: seg3()[:, :, 0:8]
                    x2 = lambda seg3=seg3: seg3()[:, :, 8:16]
                    cs = lambda t=t, nh=nh: cst[:, t, 0:nh * 8].rearrange("p (h d) -> p h d", h=nh)
                    sn = lambda t=t, nh=nh: snt[:, t, 0:nh * 8].rearrange("p (h d) -> p h d", h=nh)
                    P.op('dve', lambda e, x1=x1, cs=cs, nh=nh: e.tensor_tensor(out=t1[:, 0:nh, :], in0=x1(), in1=cs(), op=ALU.mult), reads=[ot, cst], writes=[t1])
                    P.op('dve', lambda e, x2=x2, sn=sn, nh=nh: e.tensor_tensor(out=t2[:, 0:nh, :], in0=x2(), in1=sn(), op=ALU.mult), reads=[ot, snt], writes=[t2])
                    P.op('dve', lambda e, x1=x1, sn=sn, nh=nh: e.tensor_tensor(out=t3[:, 0:nh, :], in0=x1(), in1=sn(), op=ALU.mult), reads=[ot, snt], writes=[t3])
                    P.op('dve', lambda e, x2=x2, cs=cs, nh=nh: e.tensor_tensor(out=t4[:, 0:nh, :], in0=x2(), in1=cs(), op=ALU.mult), reads=[ot, cst], writes=[t4])
                    P.op('dve', lambda e, x1=x1, nh=nh: e.tensor_tensor(out=x1(), in0=t1[:, 0:nh, :], in1=t2[:, 0:nh, :], op=ALU.subtract), reads=[t1, t2], writes=[ot])
                    P.op('dve', lambda e, x2=x2, nh=nh: e.tensor_tensor(out=x2(), in0=t3[:, 0:nh, :], in1=t4[:, 0:nh, :], op=ALU.add), reads=[t3, t4], writes=[ot])
            P.dma(y[t * 128:(t + 1) * 128, :], ot[:], reads=[ot], q='pool')
        P.finish()
    return nc


def build_out(ntok):
    nc = bass.Bass("TRN2", target_bir_lowering=False)
    x = nc.dram_tensor("x", [ntok, D], F32, kind="ExternalInput").ap()
    oT = nc.dram_tensor("oT", [D, ntok], F32, kind="ExternalInput").ap()
    gT = nc.dram_tensor("gT", [D, ntok], F32, kind="ExternalInput").ap()
    w = nc.dram_tensor("w", [D, D], F32, kind="ExternalInput").ap()
    y = nc.dram_tensor("y", [ntok, D], F32, kind="ExternalOutput").ap()
    nt = ntok // 128
    with ExitStack() as es:
        P = Prog(nc, es)
        wb = P.sbuf([128, 8, D], BF16)
        wst = [P.sbuf([128, 8, 512]) for _ in range(2)]
        wv = w.rearrange("(k p) n -> p k n", p=128)
        for ci in range(2):
            P.dma(wst[ci][:], wv[:, :, ci * 512:(ci + 1) * 512], writes=[wst[ci]], q=('sp' if ci == 0 else 'pool'))
            P.op('dve', lambda e, ci=ci: e.tensor_copy(out=wb[:, :, ci * 512:(ci + 1) * 512], in_=wst[ci][:]),
                 reads=[wst[ci]], writes=[wb])
        oTv = oT.rearrange("(k p) t -> p k t", p=128)
        gTv = gT.rearrange("(k p) t -> p k t", p=128)
        ots = [P.sbuf([128, 8, 128]) for _ in range(2)]
        gts = [P.sbuf([128, 8, 128]) for _ in range(2)]
        xts = [P.sbuf([128, D]) for _ in range(2)]
        mts = [P.sbuf([128, 8, 128], BF16) for _ in range(2)]
        yts = [P.sbuf([128, D]) for _ in range(2)]
        pos = [P.psum([128, 512]) for _ in range(4)]
        pi = 0
        for t in range(nt):
            ot = ots[t % 2]; gt = gts[t % 2]; xt = xts[t % 2]; mt = mts[t % 2]; yt = yts[t % 2]
            P.dma(ot[:], oTv[:, :, t * 128:(t + 1) * 128], writes=[ot])
            P.dma(gt[:], gTv[:, :, t * 128:(t + 1) * 128], writes=[gt], q='pool')
            P.dma(xt[:], x[t * 128:(t + 1) * 128, :], writes=[xt])
            P.op('dve', lambda e, ot=ot, gt=gt, mt=mt: e.tensor_tensor(out=mt[:], in0=ot[:], in1=gt[:], op=ALU.mult),
                 reads=[ot, gt], writes=[mt])
            for c in range(2):
                po = pos[pi % 4]; pi += 1
                for k in range(8):
                    P.op('pe', lambda e, k=k, po=po, c=c, mt=mt: e.matmul(po[:], lhsT=mt[:, k, :], rhs=wb[:, k, c * 512:(c + 1) * 512],
                                                                         start=(k == 0), stop=(k == 7)), reads=[mt, wb], writes=[po])
                P.op('dve', lambda e, po=po, c=c, xt=xt, yt=yt: e.tensor_tensor(out=yt[:, c * 512:(c + 1) * 512], in0=po[:],
                                                                               in1=xt[:, c * 512:(c + 1) * 512], op=ALU.add),
                     reads=[po, xt], writes=[yt])
            P.dma(y[t * 128:(t + 1) * 128, :], yt[:], reads=[yt], q='pool')
        P.finish()
    return nc


def build_att(jobs, lam_init=0.0):
    nc = bass.Bass("TRN2", target_bir_lowering=False)
    identd = nc.dram_tensor("ident", [128, 128], F32, kind="ExternalInput").ap()
    io = []
    for ji, job in enumerate(jobs):
        kd = job['kind']
        d = {}
        if kd == 'moba':
            d['qT'] = [nc.dram_tensor("j%d_qT" % ji, [64, 4096], F32, kind="ExternalInput").ap()]
            d['kT'] = [nc.dram_tensor("j%d_kT" % ji, [64, S], F32, kind="ExternalInput").ap()]
            d['v'] = nc.dram_tensor("j%d_v" % ji, [S, 64], F32, kind="ExternalInput").ap()
            d['masks'] = nc.dram_tensor("j%d_masks" % ji, [128, 8, 512], F32, kind="ExternalInput").ap()
            d['pm'] = nc.dram_tensor("j%d_pm" % ji, [128, 32, 32], F32, kind="ExternalInput").ap()
            d['E'] = nc.dram_tensor("j%d_E" % ji, [32, S], F32, kind="ExternalInput").ap()
            d['o'] = nc.dram_tensor("j%d_o" % ji, [4096, 64], F32, kind="ExternalOutput").ap()
        elif kd == 'diff':
            d['qT'] = [nc.dram_tensor("j%d_qT%d" % (ji, m), [64, S], F32, kind="ExternalInput").ap() for m in range(2)]
            d['kT'] = [nc.dram_tensor("j%d_kT%d" % (ji, m), [64, S], F32, kind="ExternalInput").ap() for m in range(2)]
            d['v'] = nc.dram_tensor("j%d_v" % ji, [S, 128], F32, kind="ExternalInput").ap()
            d['masks'] = nc.dram_tensor("j%d_masks" % ji, [128, 4, 512], F32, kind="ExternalInput").ap()
            d['lqk'] = nc.dram_tensor("j%d_lqk" % ji, [128, 4, 64], F32, kind="ExternalInput").ap()
            d['gsub'] = nc.dram_tensor("j%d_gsub" % ji, [128, 128], F32, kind="ExternalInput").ap()
            d['o'] = nc.dram_tensor("j%d_o" % ji, [S, 128], F32, kind="ExternalOutput").ap()
        else:
            d['qT'] = [nc.dram_tensor("j%d_qT" % ji, [64, S], F32, kind="ExternalInput").ap()]
            d['kT'] = [nc.dram_tensor("j%d_kT" % ji, [64, NMEM], F32, kind="ExternalInput").ap()]
            d['v'] = nc.dram_tensor("j%d_v" % ji, [NMEM, 64], F32, kind="ExternalInput").ap()
            d['o'] = nc.dram_tensor("j%d_o" % ji, [S, 64], F32, kind="ExternalOutput").ap()
        io.append(d)
    with ExitStack() as es:
        P = Prog(nc, es)
        idf = P.sbuf([128, 128])
        P.dma(idf[:], identd, writes=[idf])
        QTa = [P.sbuf([96, S], BF16) for _ in range(2)]
        KTa = [P.sbuf([96, S], BF16) for _ in range(2)]
        QTf = P.sbuf([64, 4096])
        stg = [P.sbuf([96, 2048]) for _ in range(2)]
        Va = P.sbuf([128, 64, 129], BF16)
        vst = [P.sbuf([128, 16, 128]) for _ in range(2)]
        mk = P.sbuf([128, 8, 512], BF16)
        mst = [P.sbuf([128, 512]) for _ in range(2)]
        pmt = P.sbuf([128, 32, 32])
        ksum = P.sbuf([64, 32])
        gm = P.sbuf([128, 32]); top8 = P.sbuf([128, 8]); thr = P.sbuf([128, 1]); sel = P.sbuf([128, 32])
        neg96 = P.sbuf([128, 96])
        pss = [P.psum([128, 512]) for _ in range(2)]
        accs = [P.psum([128, 129]) for _ in range(4)]
        pg = P.psum([128, 128])
        pts = [P.sbuf([128, 512], BF16) for _ in range(3)]
        rec = P.sbuf([128, 4])
        o1 = P.sbuf([128, 4, 128]); o2s = [P.sbuf([128, 4, 128]) for _ in range(2)]
        lq = P.sbuf([128, 4, 64]); lpr = P.sbuf([128, 2, 64]); lsum = P.sbuf([128, 2]); nlam = P.sbuf([128, 1])
        gsub = P.sbuf([128, 128]); sqd = P.sbuf([128, 128]); ssd = P.sbuf([128, 4]); rsd = P.sbuf([128, 4])
        P.op('pool', lambda e: e.memset(neg96[:], 0.0), writes=[neg96])
        sti = 0
        oi = 0
        for ji, job in enumerate(jobs):
            kd = job['kind']; d = io[ji]
            nmaps = 2 if kd == 'diff' else 1
            nq = 4096 if kd == 'moba' else S
            nk = NMEM if kd == 'mem' else S
            dv = 128 if kd == 'diff' else 64
            dq = 96 if kd == 'moba' else 64
            for m in range(nmaps):
                for c0 in range(0, nq, 2048):
                    st = stg[sti % 2]; sti += 1
                    P.dma(st[0:64, :], d['qT'][m][:, c0:c0 + 2048], writes=[st])
                    P.op('dve', lambda e, st=st, m=m, c0=c0: e.tensor_copy(out=QTa[m][0:64, c0:c0 + 2048], in_=st[0:64, :]),
                         reads=[st], writes=[QTa[m]])
                    if kd == 'moba':
                        P.op('pool', lambda e, st=st, c0=c0: e.tensor_copy(out=QTf[:, c0:c0 + 2048], in_=st[0:64, :]),
                             reads=[st], writes=[QTf])
                for c0 in range(0, nk, 2048):
                    cw = min(2048, nk - c0)
                    st = stg[sti % 2]; sti += 1
                    P.dma(st[0:64, 0:cw], d['kT'][m][:, c0:c0 + cw], writes=[st], q='pool')
                    if kd == 'moba':
                        P.dma(st[64:96, 0:cw], d['E'][:, c0:c0 + cw], writes=[st], q='pool')
                        P.op('act', lambda e, st=st, m=m, c0=c0, cw=cw: e.copy(out=KTa[m][0:96, c0:c0 + cw], in_=st[0:96, 0:cw]),
                             reads=[st], writes=[KTa[m]])
                        P.op('dve', lambda e, st=st, c0=c0: e.tensor_reduce(out=ksum[:, c0 // 256:c0 // 256 + 8],
                                                                            in_=st[0:64, :].rearrange("p (j l) -> p j l", l=256),
                                                                            axis=AX.X, op=ALU.add), reads=[st], writes=[ksum])
                    else:
                        P.op('act', lambda e, st=st, m=m, c0=c0, cw=cw: e.copy(out=KTa[m][0:64, c0:c0 + cw], in_=st[0:64, 0:cw]),
                             reads=[st], writes=[KTa[m]])
            nkt = nk // 128
            P.op('pool', lambda e, nkt=nkt, dv=dv: e.memset(Va[:, 0:nkt, dv:dv + 1], 1.0), writes=[Va])
            vv = d['v'].rearrange("(t p) c -> p t c", p=128)
            for t0 in range(0, nkt, 16):
                tw = min(16, nkt - t0)
                st = vst[sti % 2]; sti += 1
                P.dma(st[:, 0:tw, 0:dv], vv[:, t0:t0 + tw, :], writes=[st])
                P.op('dve', lambda e, st=st, t0=t0, tw=tw, dv=dv: e.tensor_copy(out=Va[:, t0:t0 + tw, 0:dv], in_=st[:, 0:tw, 0:dv]),
                     reads=[st], writes=[Va])
            nmask = {'moba': 8, 'diff': 4, 'mem': 0}[kd]
            for r in range(nmask):
                st = mst[sti % 2]; sti += 1
                P.dma(st[:], d['masks'][:, r, :], writes=[st])
                P.op('dve', lambda e, st=st, r=r: e.tensor_copy(out=mk[:, r, :], in_=st[:]), reads=[st], writes=[mk])
            if kd == 'moba':
                P.dma(pmt[:], d['pm'], writes=[pmt])
                for qt in range(32):
                    P.op('pe', lambda e, qt=qt: e.matmul(pg[:, 0:32], lhsT=QTf[:, qt * 128:(qt + 1) * 128], rhs=ksum[:], start=True, stop=True),
                         reads=[QTf, ksum], writes=[pg])
                    P.op('dve', lambda e, qt=qt: e.tensor_tensor(out=gm[:], in0=pg[:, 0:32], in1=pmt[:, qt, :], op=ALU.add),
                         reads=[pg, pmt], writes=[gm])
                    P.op('dve', lambda e: e.max(out=top8[:], in_=gm[:]), reads=[gm], writes=[top8])
                    P.op('dve', lambda e: e.tensor_scalar(out=thr[:], in0=top8[:, 3:4], scalar1=-1e29, scalar2=None, op0=ALU.max),
                         reads=[top8], writes=[thr])
                    P.op('dve', lambda e: e.tensor_scalar(out=sel[:], in0=gm[:], scalar1=thr[:, 0:1], scalar2=None, op0=ALU.is_ge),
                         reads=[gm, thr], writes=[sel])
                    P.op('dve', lambda e: e.tensor_scalar(out=neg96[:, 64:96], in0=sel[:], scalar1=30000.0, scalar2=-30000.0,
                                                          op0=ALU.mult, op1=ALU.add), reads=[sel], writes=[neg96])
                    P.op('pe', lambda e: e.matmul(pg[0:96, :], lhsT=neg96[:], rhs=idf[:], start=True, stop=True),
                         reads=[neg96, idf], writes=[pg])
                    P.op('act', lambda e, qt=qt: e.copy(out=QTa[0][64:96, qt * 128:(qt + 1) * 128], in_=pg[64:96, :]),
                         reads=[pg], writes=[QTa[0]])
            if kd == 'diff':
                P.dma(lq[:], d['lqk'], writes=[lq])
                P.dma(gsub[:], d['gsub'], writes=[gsub])
                P.op('dve', lambda e: e.tensor_tensor(out=lpr[:, 0, :], in0=lq[:, 0, :], in1=lq[:, 1, :], op=ALU.mult), reads=[lq], writes=[lpr])
                P.op('dve', lambda e: e.tensor_tensor(out=lpr[:, 1, :], in0=lq[:, 2, :], in1=lq[:, 3, :], op=ALU.mult), reads=[lq], writes=[lpr])
                P.op('dve', lambda e: e.tensor_reduce(out=lsum[:], in_=lpr[:], axis=AX.X, op=ALU.add), reads=[lpr], writes=[lsum])
                P.op('act', lambda e: e.activation(out=lsum[:], in_=lsum[:], func=AF.Exp), reads=[lsum], writes=[lsum])
                P.op('dve', lambda e: e.tensor_tensor(out=nlam[:], in0=lsum[:, 1:2], in1=lsum[:, 0:1], op=ALU.subtract), reads=[lsum], writes=[nlam])
                P.op('dve', lambda e: e.tensor_scalar(out=nlam[:], in0=nlam[:], scalar1=-float(lam_init), scalar2=None, op0=ALU.add),
                     reads=[nlam], writes=[nlam])
            ngroups = nq // 512
            ovw = d['o'].rearrange("(g t p) c -> g p t c", p=128, t=4)
            si = 0
            for g in range(ngroups):
                if kd == 'moba':
                    kts = list(range(8 * g + 8)); first_masked = 8 * g
                elif kd == 'diff':
                    kts = list(range(4 * g + 4)); first_masked = 4 * g
                else:
                    kts = [0, 1]; first_masked = 99999
                q0 = g * 512
                for m in range(nmaps):
                    for kt in kts:
                        ps = pss[si % 2]; pt = pts[si % 3]; si += 1
                        P.op('pe', lambda e, ps=ps, m=m, kt=kt, q0=q0, dq=dq: e.matmul(
                            ps[:], lhsT=KTa[m][0:dq, kt * 128:(kt + 1) * 128], rhs=QTa[m][0:dq, q0:q0 + 512], start=True, stop=True),
                            reads=[KTa[m], QTa[m]], writes=[ps])
                        P.op('act', lambda e, ps=ps, pt=pt: e.activation(out=pt[:], in_=ps[:], func=AF.Exp, scale=0.125),
                             reads=[ps], writes=[pt])
                        r = kt - first_masked
                        if r >= 0:
                            P.op('dve', lambda e, pt=pt, r=r: e.tensor_tensor(out=pt[:], in0=pt[:], in1=mk[:, r, :], op=ALU.mult),
                                 reads=[pt, mk], writes=[pt])
                        for qt in range(4):
                            if kd == 'diff' and r > qt:
                                continue
                            last = (kt == (4 * g + qt)) if kd == 'diff' else (kt == kts[-1])
                            P.op('pe', lambda e, pt=pt, qt=qt, kt=kt, dv=dv, last=last: e.matmul(
                                accs[qt][:, 0:dv + 1], lhsT=pt[:, qt * 128:(qt + 1) * 128], rhs=Va[:, kt, 0:dv + 1],
                                start=(kt == 0), stop=last), reads=[pt, Va], writes=[accs[qt]])
                    ot = o1 if (kd == 'diff' and m == 0) else o2s[oi % 2]
                    for qt in range(4):
                        P.op('dve', lambda e, qt=qt, dv=dv: e.reciprocal(out=rec[:, qt:qt + 1], in_=accs[qt][:, dv:dv + 1]),
                             reads=[accs[qt]], writes=[rec])
                        P.op('dve', lambda e, qt=qt, dv=dv, ot=ot: e.tensor_scalar(out=ot[:, qt, 0:dv], in0=accs[qt][:, 0:dv],
                                                                                    scalar1=rec[:, qt:qt + 1], scalar2=None, op0=ALU.mult),
                             reads=[accs[qt], rec], writes=[ot])
                if kd == 'diff':
                    ot = o2s[oi % 2]
                    for qt in range(4):
                        P.op('dve', lambda e, qt=qt, ot=ot: e.scalar_tensor_tensor(out=ot[:, qt, :], in0=ot[:, qt, :], scalar=nlam[:, 0:1],
                                                                                   in1=o1[:, qt, :], op0=ALU.mult, op1=ALU.add),
                             reads=[ot, nlam, o1], writes=[ot])
                        P.op('act', lambda e, qt=qt, ot=ot: e.activation(out=sqd[:], in_=ot[:, qt, :], func=AF.Square, accum_out=ssd[:, qt:qt + 1]),
                             reads=[ot], writes=[sqd, ssd])
                    _rsqrt_mean(P, rsd, ssd, 128)
                    for qt in range(4):
                        P.op('dve', lambda e, qt=qt, ot=ot: e.tensor_scalar(out=ot[:, qt, :], in0=ot[:, qt, :], scalar1=rsd[:, qt:qt + 1],
                                                                            scalar2=float(1.0 - lam_init), op0=ALU.mult, op1=ALU.mult),
                             reads=[ot, rsd], writes=[ot])
                        P.op('pool', lambda e, qt=qt, ot=ot: e.tensor_tensor(out=ot[:, qt, :], in0=ot[:, qt, :], in1=gsub[:], op=ALU.mult),
                             reads=[ot, gsub], writes=[ot])
                ot = o2s[oi % 2]; oi += 1
                P.dma(ovw[g], ot[:, :, 0:dv], reads=[ot], q='pool')
        P.finish()
    return nc


SEG = 1024


def build_hgrn(jm, dbg_nseg=None, dbg_ntiles=None, dbg_stage=9):
    nc = bass.Bass("TRN2", target_bir_lowering=False)
    din = lambda n, s: nc.dram_tensor(n, s, F32, kind="ExternalInput").ap()
    qTd = din("qT", [64, S]); fTd = din("fT", [64, S]); vd = din("v", [S, 64])
    dlbd = din("dlb", [64, 2]); gngd = din("gng", [128, 64])
    resetd = din("resetm", [64, SEG]); cmAd = din("cmA", [64, SEG]); cmBd = din("cmB", [64, SEG])
    tmaskd = din("tmaskT", [128, 128]); identd = din("ident", [128, 128])
    od = nc.dram_tensor("o", [S, 64], F32, kind="ExternalOutput").ap()
    nseg = S // SEG; tps = SEG // 128
    if dbg_nseg is not None:
        nseg = dbg_nseg
    tps_run = tps if dbg_ntiles is None else dbg_ntiles
    with ExitStack() as es:
        P = Prog(nc, es)
        idf = P.sbuf([128, 128]); tmask = P.sbuf([128, 128]); gng = P.sbuf([128, 64])
        resetm = P.sbuf([64, SEG]); cmA = P.sbuf([64, SEG]); cmB = P.sbuf([64, SEG])
        dlb = P.sbuf([64, 2]); lb = P.sbuf([64, 1]); oml = P.sbuf([64, 1])
        Vall = P.sbuf([128, S // 128, 64])
        for tdst, src in [(idf, identd), (tmask, tmaskd), (gng, gngd), (resetm, resetd), (cmA, cmAd), (cmB, cmBd), (dlb, dlbd)]:
            P.dma(tdst[:], src, writes=[tdst])
        vdv = vd.rearrange("(t p) c -> p t c", p=128)
        for t0 in range(0, S // 128, 8):
            P.dma(Vall[:, t0:t0 + 8, :], vdv[:, t0:t0 + 8, :], writes=[Vall])
        V64 = P.sbuf([64, S // 64, 64])
        vdc = vd.rearrange("(c p) d -> p c d", p=64)
        for t0 in range(0, S // 64, 16):
            P.dma(V64[:, t0:t0 + 16, :], vdc[:, t0:t0 + 16, :], writes=[V64], q='pool')
        P.op('dve', lambda e: e.tensor_tensor(out=lb[:], in0=dlb[:, 1:2], in1=dlb[:, 0:1], op=ALU.subtract), reads=[dlb], writes=[lb])
        P.op('act', lambda e: e.activation(out=lb[:], in_=lb[:], func=AF.Sigmoid), reads=[lb], writes=[lb])
        P.op('dve', lambda e: e.tensor_scalar(out=lb[:], in0=lb[:], scalar1=float(jm), scalar2=None, op0=ALU.mult), reads=[lb], writes=[lb])
        P.op('dve', lambda e: e.tensor_scalar(out=oml[:], in0=lb[:], scalar1=-1.0, scalar2=1.0, op0=ALU.mult, op1=ALU.add), reads=[lb], writes=[oml])
        St = [P.sbuf([64, 64]) for _ in range(S // 64 + 1)]
        P.op('pool', lambda e: e.memset(St[0][:], 0.0), writes=[St[0]])
        mk = lambda: [P.sbuf([64, SEG]) for _ in range(2)]
        qs, fs, ks, lfs, bc, eb, enb, qtl, qA, qB, ktl, kh = [mk() for _ in range(12)]
        pbank = [P.psum([128, 512]) for _ in range(8)]
        R = [dict(pk=pbank[0 + q], ps=pbank[2 + q], pd=pbank[4 + q], po=pbank[6 + q]) for q in range(2)]
        khT = [P.sbuf([64, 128]) for _ in range(2)]
        scT = [P.sbuf([128, 128]) for _ in range(2)]
        ss = [P.sbuf([128, 1]) for _ in range(2)]; rs = [P.sbuf([128, 1]) for _ in range(2)]
        sqj = P.sbuf([128, 64])
        ot = [P.sbuf([128, 64]) for _ in range(2)]
        for sg in range(nseg):
            p = sg % 2; c0 = sg * SEG
            P.dma(qs[p][:], qTd[:, c0:c0 + SEG], writes=[qs[p]])
            P.dma(fs[p][:], fTd[:, c0:c0 + SEG], writes=[fs[p]], q='pool')
            P.op('act', lambda e, p=p: e.activation(out=fs[p][:], in_=fs[p][:], func=AF.Sigmoid), reads=[fs[p]], writes=[fs[p]])
            P.op('dve', lambda e, p=p: e.tensor_scalar(out=fs[p][:], in0=fs[p][:], scalar1=oml[:, 0:1], scalar2=lb[:, 0:1], op0=ALU.mult, op1=ALU.add),
                 reads=[fs[p], oml, lb], writes=[fs[p]])
            P.op('pool', lambda e, p=p: e.tensor_scalar(out=ks[p][:], in0=fs[p][:], scalar1=-1.0, scalar2=1.0, op0=ALU.mult, op1=ALU.add),
                 reads=[fs[p]], writes=[ks[p]])
            P.op('act', lambda e, p=p: e.activation(out=lfs[p][:], in_=fs[p][:], func=AF.Ln), reads=[fs[p]], writes=[lfs[p]])
            P.op('dve', lambda e, p=p: e.tensor_tensor_scan(out=bc[p][:], data0=resetm[:], data1=lfs[p][:], initial=0.0, op0=ALU.mult, op1=ALU.add),
                 reads=[resetm, lfs[p]], writes=[bc[p]])
            P.op('act', lambda e, p=p: e.activation(out=eb[p][:], in_=bc[p][:], func=AF.Exp), reads=[bc[p]], writes=[eb[p]])
            P.op('act', lambda e, p=p: e.activation(out=enb[p][:], in_=bc[p][:], func=AF.Exp, scale=-1.0), reads=[bc[p]], writes=[enb[p]])
            P.op('dve', lambda e, p=p: e.tensor_tensor(out=qtl[p][:], in0=qs[p][:], in1=eb[p][:], op=ALU.mult), reads=[qs[p], eb[p]], writes=[qtl[p]])
            P.op('pool', lambda e, p=p: e.tensor_tensor(out=qA[p][:], in0=qtl[p][:], in1=cmA[:], op=ALU.mult), reads=[qtl[p], cmA], writes=[qA[p]])
            P.op('pool', lambda e, p=p: e.tensor_tensor(out=qB[p][:], in0=qtl[p][:], in1=cmB[:], op=ALU.mult), reads=[qtl[p], cmB], writes=[qB[p]])
            P.op('dve', lambda e, p=p: e.tensor_tensor(out=ktl[p][:], in0=ks[p][:], in1=enb[p][:], op=ALU.mult), reads=[ks[p], enb[p]], writes=[ktl[p]])
            P.op('dve', lambda e, p=p: e.tensor_tensor(
                out=kh[p][:].rearrange("p (c l) -> p c l", l=64), in0=ktl[p][:].rearrange("p (c l) -> p c l", l=64),
                in1=eb[p][:].rearrange("p (c l) -> p c l", l=64)[:, :, 63:64].to_broadcast([64, SEG // 64, 64]), op=ALU.mult),
                reads=[ktl[p], eb[p]], writes=[kh[p]])
            for ti in range(tps_run):
                gi = sg * tps + ti; tp = gi % 2; r = R[tp]
                cs = slice(ti * 128, (ti + 1) * 128)
                for h in range(2):
                    P.op('pe', lambda e, r=r, p=p, ti=ti, h=h: e.transpose(out=r['pk'][0:64, h * 64:(h + 1) * 64],
                                                                          in_=kh[p][:, ti * 128 + h * 64:ti * 128 + (h + 1) * 64],
                                                                          identity=idf[0:64, 0:64]), reads=[kh[p], idf], writes=[r['pk']])
                P.op('act', lambda e, r=r, tp=tp: e.copy(out=khT[tp][:], in_=r['pk'][0:64, 0:128]), reads=[r['pk']], writes=[khT[tp]])
                if dbg_stage < 2:
                    continue
                P.op('pe', lambda e, r=r, p=p, cs=cs: e.matmul(r['ps'][:, 0:128], lhsT=ktl[p][:, cs], rhs=qtl[p][:, cs], start=True, stop=True),
                     reads=[ktl[p], qtl[p]], writes=[r['ps']])
                P.op('dve', lambda e, r=r, tp=tp: e.tensor_tensor(out=scT[tp][:], in0=r['ps'][:, 0:128], in1=tmask[:], op=ALU.mult),
                     reads=[r['ps'], tmask], writes=[scT[tp]])
                if dbg_stage < 3:
                    continue
                pd = r['pd']
                for h in range(2):
                    P.op('pe', lambda e, pd=pd, tp=tp, h=h, gi=gi: e.matmul(pd[0:64, h * 64:(h + 1) * 64], lhsT=khT[tp][:, h * 64:(h + 1) * 64],
                                                                             rhs=V64[:, 2 * gi + h, :], start=True, stop=True),
                         reads=[khT[tp], V64], writes=[pd])
                for h in range(2):
                    c = gi * 2 + h
                    ce = ti * 128 + h * 64 + 63
                    P.op('dve', lambda e, pd=pd, c=c, p=p, ce=ce, h=h: e.scalar_tensor_tensor(
                        out=St[c + 1][:], in0=St[c][:], scalar=eb[p][:, ce:ce + 1], in1=pd[0:64, h * 64:(h + 1) * 64], op0=ALU.mult, op1=ALU.add),
                        reads=[St[c], eb[p], pd], writes=[St[c + 1]])
                if dbg_stage < 4:
                    continue
                po = r['po']
                P.op('pe', lambda e, po=po, tp=tp, gi=gi: e.matmul(po[:, 0:64], lhsT=scT[tp][:], rhs=Vall[:, gi, :], start=True, stop=False),
                     reads=[scT[tp], Vall], writes=[po])
                P.op('pe', lambda e, po=po, p=p, cs=cs, gi=gi: e.matmul(po[:, 0:64], lhsT=qA[p][:, cs], rhs=St[2 * gi][:], start=False, stop=False),
                     reads=[qA[p], St[2 * gi]], writes=[po])
                P.op('pe', lambda e, po=po, p=p, cs=cs, gi=gi: e.matmul(po[:, 0:64], lhsT=qB[p][:, cs], rhs=St[2 * gi + 1][:], start=False, stop=True),
                     reads=[qB[p], St[2 * gi + 1]], writes=[po])
                if dbg_stage < 5:
                    continue
                P.op('act', lambda e, po=po, tp=tp: e.activation(out=sqj[:], in_=po[:, 0:64], func=AF.Square, accum_out=ss[tp][:]),
                     reads=[po], writes=[sqj, ss[tp]])
                _rsqrt_mean(P, rs[tp], ss[tp], 64)
                P.op('dve', lambda e, po=po, tp=tp: e.scalar_tensor_tensor(out=ot[tp][:], in0=po[:, 0:64], scalar=rs[tp][:, 0:1], in1=gng[:],
                                                                           op0=ALU.mult, op1=ALU.mult), reads=[po, rs[tp], gng], writes=[ot[tp]])
                P.dma(od[gi * 128:(gi + 1) * 128, :], ot[tp][:], reads=[ot[tp]], q='pool')
        P.finish()
    return nc


def hgrn_consts():
    t = np.arange(SEG)
    resetm = np.tile(((t % 64) != 0).astype(np.float32)[None], (64, 1))
    cmA = np.tile(((t // 64) % 2 == 0).astype(np.float32)[None], (64, 1))
    cmB = np.tile(((t // 64) % 2 == 1).astype(np.float32)[None], (64, 1))
    s = np.arange(128)[:, None]; q = np.arange(128)[None, :]
    tmaskT = ((s <= q) & (s // 64 == q // 64)).astype(np.float32)
    return dict(resetm=resetm, cmA=cmA, cmB=cmB, tmaskT=tmaskT, ident=np.eye(128, dtype=np.float32))


RWKV_GN_EPS = 64e-5
NEG_EXP_HALF = -0.6065306597126334


RSEG = 512


def build_rwkv(njobs=2, dbg_nseg=None, dbg_stage=99, dbg_neu=99, dbg_ntiles=None, dbg_t0=0, serial=True):
    nc = bass.Bass("TRN2", target_bir_lowering=False)
    din = lambda n, s: nc.dram_tensor(n, s, F32, kind="ExternalInput").ap()
    resetd = din("resetm", [64, RSEG]); cmAd = din("cmA", [64, RSEG]); cmBd = din("cmB", [64, RSEG])
    mT4d = din("mT4", [128, 512]); mstd = din("mstrict", [128, 128]); identd = din("ident", [128, 128]); onesd = din("ones64", [64, 64])
    jio = []
    for j in range(njobs):
        d = dict(x3=din("j%d_x3" % j, [3, 64, S + 1]), xl=din("j%d_xl" % j, [2, 32, S + 1]), pv=din("j%d_pv" % j, [64, 8]),
                 pl=din("j%d_pl" % j, [32, 2]), w2=din("j%d_w2" % j, [32, 64]), a2=din("j%d_a2" % j, [32, 64]),
                 lnx=din("j%d_lnx" % j, [128, 2, 64]),
                 o=nc.dram_tensor("j%d_o" % j, [S, 64], F32, kind="ExternalOutput").ap())
        jio.append(d)
    nseg = S // RSEG if dbg_nseg is None else dbg_nseg
    tps = RSEG // 128
    with ExitStack() as es:
        P = Prog(nc, es)
        resetm = P.sbuf([64, RSEG]); cmA = P.sbuf([64, RSEG]); cmB = P.sbuf([64, RSEG])
        mT4 = P.sbuf([128, 512]); mst = P.sbuf([128, 128]); idf = P.sbuf([128, 128]); ones64 = P.sbuf([64, 64])
        for tdst, src in [(resetm, resetd), (cmA, cmAd), (cmB, cmBd), (mT4, mT4d), (mst, mstd), (idf, identd), (ones64, onesd)]:
            P.dma(tdst[:], src, writes=[tdst])
        pv = P.sbuf([64, 8]); pl = P.sbuf([32, 2]); w2 = P.sbuf([32, 64]); a2 = P.sbuf([32, 64]); lnx = P.sbuf([128, 2, 64])
        omka = P.sbuf([64, 1])
        X3 = P.sbuf([64, 3, RSEG + 1]); XL = P.sbuf([32, 2, RSEG + 1])
        F = lambda: P.sbuf([64, RSEG])
        rs, ks, vs, tmp, lw, asig, kk, kkn, kmod, bvec, cum, g, gi, gprev = [F() for _ in range(14)]
        ah, rh, rhA, rhB, bt, ktl, bcA, bcB, kcA, kcB, rkr = [F() for _ in range(11)]
        wls = P.sbuf([32, RSEG]); als = P.sbuf([32, RSEG]); tw = P.sbuf([32, RSEG]); tl = P.sbuf([32, RSEG])
        bank = [P.psum([128, 512]) for _ in range(8)]
        ST = [P.sbuf([64, 64]) for _ in range(S // 64 + 1)]
        tm = [P.sbuf([128, 384]) for _ in range(2)]
        G = [P.sbuf([128, 512]) for _ in range(2)]
        Lps = [P.sbuf([128, 128]) for _ in range(2)]
        Pbs = [[P.sbuf([128, 128]) for _ in range(2)] for _ in range(2)]
        PTbs = [[P.sbuf([128, 128]) for _ in range(2)] for _ in range(2)]
        Xbs = [[P.sbuf([128, 128]) for _ in range(2)] for _ in range(2)]
        WATas = [P.sbuf([64, 128]) for _ in range(2)]; WATbs = [P.sbuf([64, 128]) for _ in range(2)]
        MTs = [[P.sbuf([64, 64]) for _ in range(2)] for _ in range(2)]
        Uts = [P.sbuf([128, 64]) for _ in range(2)]
        s1 = P.sbuf([128, 1]); nmean = P.sbuf([128, 1]); ym = P.sbuf([128, 64]); sqj = P.sbuf([128, 64])
        s2 = P.sbuf([128, 1]); rstd = P.sbuf([128, 1]); yo = P.sbuf([128, 64]); pbs = P.sbuf([128, 1])
        ots = [P.sbuf([128, 64]) for _ in range(2)]
        for q_ in range(2):
            P.op('pool', lambda e, q_=q_: e.memset(WATas[q_][:], 0.0), writes=[WATas[q_]])
            P.op('pool', lambda e, q_=q_: e.memset(WATbs[q_][:], 0.0), writes=[WATbs[q_]])
        P.op('pool', lambda e: e.memset(ST[0][:], 0.0), writes=[ST[0]])
        tt = lambda eng, o, a, b, op, rd, wr: P.op(eng, lambda e: e.tensor_tensor(out=o, in0=a, in1=b, op=op), reads=rd, writes=wr)
        oi = 0
        for j in range(njobs):
            d = jio[j]
            for tdst, src in [(pv, d['pv']), (pl, d['pl']), (w2, d['w2']), (a2, d['a2']), (lnx, d['lnx'])]:
                P.dma(tdst[:], src, writes=[tdst])
            P.op('dve', lambda e: e.tensor_scalar(out=omka[:], in0=pv[:, 6:7], scalar1=-1.0, scalar2=1.0, op0=ALU.mult, op1=ALU.add),
                 reads=[pv], writes=[omka])
            for sg in range(nseg):
                c0 = sg * RSEG
                for a in range(3):
                    P.dma(X3[:, a, :], d['x3'][a, :, c0:c0 + RSEG + 1], writes=[X3], q=('sp' if a != 1 else 'pool'))
                for a in range(2):
                    P.dma(XL[:, a, :], d['xl'][a, :, c0:c0 + RSEG + 1], writes=[XL], q='pool')
                for a, dst in enumerate([rs, ks, vs]):
                    tt('dve', tmp[:], X3[:, a, 0:RSEG], X3[:, a, 1:RSEG + 1], ALU.subtract, [X3], [tmp])
                    P.op('dve', lambda e, a=a, dst=dst: e.scalar_tensor_tensor(out=dst[:], in0=tmp[:], scalar=pv[:, a:a + 1], in1=X3[:, a, 1:RSEG + 1],
                                                                              op0=ALU.mult, op1=ALU.add), reads=[tmp, pv, X3], writes=[dst])
                for a, dst in enumerate([wls, als]):
                    tt('dve', tl[:], XL[:, a, 0:RSEG], XL[:, a, 1:RSEG + 1], ALU.subtract, [XL], [tl])
                    P.op('dve', lambda e, a=a, dst=dst: e.scalar_tensor_tensor(out=dst[:], in0=tl[:], scalar=pl[:, a:a + 1], in1=XL[:, a, 1:RSEG + 1],
                                                                              op0=ALU.mult, op1=ALU.add), reads=[tl, pl, XL], writes=[dst])
                P.op('act', lambda e: e.activation(out=tw[:], in_=wls[:], func=AF.Tanh), reads=[wls], writes=[tw])
                for hlf in range(RSEG // 512):
                    cs = slice(hlf * 512, (hlf + 1) * 512)
                    P.op('pe', lambda e, cs=cs: e.matmul(bank[0][0:64, :], lhsT=w2[:], rhs=tw[:, cs], start=True, stop=True), reads=[w2, tw], writes=[bank[0]])
                    P.op('act', lambda e, cs=cs: e.activation(out=lw[:, cs], in_=bank[0][0:64, :], func=AF.Sigmoid, bias=pv[:, 3:4]),
                         reads=[bank[0], pv], writes=[lw])
                    P.op('pe', lambda e, cs=cs: e.matmul(bank[1][0:64, :], lhsT=a2[:], rhs=als[:, cs], start=True, stop=True), reads=[a2, als], writes=[bank[1]])
                    P.op('act', lambda e, cs=cs: e.activation(out=asig[:, cs], in_=bank[1][0:64, :], func=AF.Sigmoid, bias=pv[:, 4:5]),
                         reads=[bank[1], pv], writes=[asig])
                P.op('dve', lambda e: e.tensor_scalar(out=lw[:], in0=lw[:], scalar1=NEG_EXP_HALF, scalar2=None, op0=ALU.mult), reads=[lw], writes=[lw])
                P.op('dve', lambda e: e.tensor_scalar(out=kk[:], in0=ks[:], scalar1=pv[:, 5:6], scalar2=None, op0=ALU.mult), reads=[ks, pv], writes=[kk])
                P.op('act', lambda e: e.activation(out=tmp[:], in_=kk[:], func=AF.Square), reads=[kk], writes=[tmp])
                for hlf in range(RSEG // 512):
                    cs = slice(hlf * 512, (hlf + 1) * 512)
                    P.op('pe', lambda e, cs=cs: e.matmul(bank[2][0:64, :], lhsT=ones64[:], rhs=tmp[:, cs], start=True, stop=True), reads=[ones64, tmp], writes=[bank[2]])
                    P.op('dve', lambda e, cs=cs: e.tensor_scalar(out=kkn[:, cs], in0=bank[2][0:64, :], scalar1=1e-24, scalar2=None, op0=ALU.max),
                         reads=[bank[2]], writes=[kkn])
                P.op('act', lambda e: e.activation(out=kkn[:], in_=kkn[:], func=AF.Sqrt), reads=[kkn], writes=[kkn])
                P.op('dve', lambda e: e.reciprocal(out=kkn[:], in_=kkn[:]), reads=[kkn], writes=[kkn])
                tt('dve', kkn[:], kkn[:], kk[:], ALU.mult, [kkn, kk], [kkn])
                P.op('dve', lambda e: e.tensor_scalar(out=tmp[:], in0=asig[:], scalar1=pv[:, 6:7], scalar2=omka[:, 0:1], op0=ALU.mult, op1=ALU.add),
                     reads=[asig, pv, omka], writes=[tmp])
                tt('dve', kmod[:], ks[:], tmp[:], ALU.mult, [ks, tmp], [kmod])
                tt('pool', bvec[:], kkn[:], asig[:], ALU.mult, [kkn, asig], [bvec])
                P.op('dve', lambda e: e.tensor_tensor_scan(out=cum[:], data0=resetm[:], data1=lw[:], initial=0.0, op0=ALU.mult, op1=ALU.add),
                     reads=[resetm, lw], writes=[cum])
                P.op('act', lambda e: e.activation(out=g[:], in_=cum[:], func=AF.Exp), reads=[cum], writes=[g])
                P.op('act', lambda e: e.activation(out=gi[:], in_=cum[:], func=AF.Exp, scale=-1.0), reads=[cum], writes=[gi])
                tt('dve', tmp[:], cum[:], lw[:], ALU.subtract, [cum, lw], [tmp])
                P.op('act', lambda e: e.activation(out=gprev[:], in_=tmp[:], func=AF.Exp), reads=[tmp], writes=[gprev])
                P.op('dve', lambda e: e.scalar_tensor_tensor(out=ah[:], in0=kkn[:], scalar=-1.0, in1=gprev[:], op0=ALU.mult, op1=ALU.mult),
                     reads=[kkn, gprev], writes=[ah])
                tt('dve', rh[:], rs[:], g[:], ALU.mult, [rs, g], [rh])
                tt('pool', rhA[:], rh[:], cmA[:], ALU.mult, [rh, cmA], [rhA])
                tt('pool', rhB[:], rh[:], cmB[:], ALU.mult, [rh, cmB], [rhB])
                tt('dve', bt[:], bvec[:], gi[:], ALU.mult, [bvec, gi], [bt])
                tt('dve', ktl[:], kmod[:], gi[:], ALU.mult, [kmod, gi], [ktl])
                v3 = lambda t: t[:].rearrange("p (c l) -> p c l", l=64)
                gC = lambda: v3(g)[:, :, 63:64].to_broadcast([64, RSEG // 64, 64])
                P.op('dve', lambda e: e.tensor_tensor(out=v3(tmp), in0=v3(bt), in1=gC(), op=ALU.mult), reads=[bt, g], writes=[tmp])
                tt('pool', bcA[:], tmp[:], cmA[:], ALU.mult, [tmp, cmA], [bcA])
                tt('pool', bcB[:], tmp[:], cmB[:], ALU.mult, [tmp, cmB], [bcB])
                P.op('dve', lambda e: e.tensor_tensor(out=v3(kk), in0=v3(ktl), in1=gC(), op=ALU.mult), reads=[ktl, g], writes=[kk])
                tt('pool', kcA[:], kk[:], cmA[:], ALU.mult, [kk, cmA], [kcA])
                tt('pool', kcB[:], kk[:], cmB[:], ALU.mult, [kk, cmB], [kcB])
                P.op('dve', lambda e: e.scalar_tensor_tensor(out=rkr[:], in0=rs[:], scalar=pv[:, 7:8], in1=kmod[:], op0=ALU.mult, op1=ALU.mult),
                     reads=[rs, pv, kmod], writes=[rkr])
                P.serial = serial
                for ti in range(dbg_t0, tps if dbg_ntiles is None else dbg_ntiles):
                    gti = sg * tps + ti; q = gti % 2
                    Lp = Lps[0]; Xb = Xbs[0]; WATa = WATas[0]; WATb = WATbs[0]; MT = MTs[0]; Ut = Uts[0]
                    cs = slice(ti * 128, (ti + 1) * 128)
                    c_0 = 2 * gti
                    if dbg_stage < 1:
                        continue
                    for k, src in enumerate([vs, ah, bcA, bcB, kcA, kcB]):
                        P.op('pe', lambda e, k=k, src=src, cs=cs: e.transpose(out=bank[0][:, k * 64:(k + 1) * 64], in_=src[:, cs], identity=idf[0:64, 0:64]),
                             reads=[src, idf], writes=[bank[0]])
                    tmq = tm[q]
                    P.op('act', lambda e, tmq=tmq: e.copy(out=tmq[:], in_=bank[0][:, 0:384]), reads=[bank[0]], writes=[tmq])
                    V_ = lambda tmq=tmq: tmq[:, 0:64]
                    if dbg_stage < 2:
                        continue
                    for k, (lh, rh_) in enumerate([(bt, ah), (bt, rh), (ktl, ah), (ktl, rh)]):
                        P.op('pe', lambda e, k=k, lh=lh, rh_=rh_, cs=cs: e.matmul(bank[1][:, k * 128:(k + 1) * 128], lhsT=lh[:, cs], rhs=rh_[:, cs], start=True, stop=True),
                             reads=[lh, rh_], writes=[bank[1]])
                    Gq = G[q]
                    tt('dve', Gq[:], bank[1][:], mT4[:], ALU.mult, [bank[1], mT4], [Gq])
                    if dbg_stage < 3:
                        continue
                    P.op('pe', lambda e, cs=cs: e.matmul(bank[2][:, 0:128], lhsT=ah[:, cs], rhs=bt[:, cs], start=True, stop=True), reads=[ah, bt], writes=[bank[2]])
                    P.op('pe', lambda e, Gq=Gq, tmq=tmq: e.matmul(bank[2][:, 128:192], lhsT=Gq[:, 256:384], rhs=tmq[:, 0:64], start=True, stop=True),
                         reads=[Gq, tmq], writes=[bank[2]])
                    tt('dve', Lp[:], bank[2][:, 0:128], mst[:], ALU.mult, [bank[2], mst], [Lp])
                    X = Xb[0]
                    P.op('pool', lambda e, tmq=tmq, X=X: e.tensor_copy(out=X[:, 0:64], in_=tmq[:, 64:128]), reads=[tmq], writes=[X])
                    P.op('act', lambda e, X=X: e.copy(out=X[:, 64:128], in_=bank[2][:, 128:192]), reads=[bank[2]], writes=[X])
                    if dbg_stage < 4:
                        continue
                    for i in range(6):
                        if dbg_neu < 90 and i >= dbg_neu:
                            break
                        P.op('pe', lambda e, Gq=Gq, X=X: e.matmul(bank[3][:, 0:128], lhsT=Gq[:, 0:128], rhs=X[:], start=True, stop=True),
                             reads=[Gq, X], writes=[bank[3]])
                        if i < 5:
                            P.op('pe', lambda e, Gq=Gq, Lp=Lp: e.matmul(bank[5][:, 0:128], lhsT=Lp[:], rhs=Gq[:, 0:128], start=True, stop=True),
                                 reads=[Lp, Gq], writes=[bank[5]])
                        if i < 4:
                            P.op('pe', lambda e, Gq=Gq, Lp=Lp: e.matmul(bank[4][:, 0:128], lhsT=Gq[:, 0:128], rhs=Lp[:], start=True, stop=True),
                                 reads=[Lp, Gq], writes=[bank[4]])
                        tt('dve', X[:], X[:], bank[3][:, 0:128], ALU.add, [X, bank[3]], [X])
                        if i < 5:
                            P.op('act', lambda e, Gq=Gq: e.copy(out=Gq[:, 0:128], in_=bank[5][:, 0:128]), reads=[bank[5]], writes=[Gq])
                        if i < 4:
                            P.op('dve', lambda e, Lp=Lp: e.tensor_copy(out=Lp[:], in_=bank[4][:, 0:128]), reads=[bank[4]], writes=[Lp])
                    W = X
                    if dbg_stage < 5:
                        continue
                    P.op('pe', lambda e, W=W: e.transpose(out=bank[6][0:64, 0:128], in_=W[:, 0:64], identity=idf[:]), reads=[W, idf], writes=[bank[6]])
                    P.op('act', lambda e, WATa=WATa: e.copy(out=WATa[:, 0:64], in_=bank[6][0:64, 0:64]), reads=[bank[6]], writes=[WATa])
                    P.op('act', lambda e, WATb=WATb: e.copy(out=WATb[:, 64:128], in_=bank[6][0:64, 64:128]), reads=[bank[6]], writes=[WATb])
                    if dbg_stage < 6:
                        continue
                    for h in range(2):
                        P.op('pe', lambda e, W=W, tmq=tmq, h=h: e.matmul(bank[6][0:64, 128 + h * 64:192 + h * 64], lhsT=W[:, 0:64],
                                                                         rhs=tmq[:, 128 + h * 64:192 + h * 64], start=True, stop=True),
                             reads=[W, tmq], writes=[bank[6]])
                    for h in range(2):
                        ce = ti * 128 + h * 64 + 63
                        P.op('dve', lambda e, h=h, ce=ce, MT=MT: e.scalar_tensor_tensor(out=MT[h][:], in0=idf[0:64, 0:64], scalar=g[:, ce:ce + 1],
                                                                                in1=bank[6][0:64, 128 + h * 64:192 + h * 64], op0=ALU.mult, op1=ALU.add),
                             reads=[idf, g, bank[6]], writes=[MT[h]])
                    if dbg_stage < 7:
                        continue
                    for h in range(2):
                        c = c_0 + h
                        ps = lambda h=h: bank[7][0:64, 256 + h * 64:320 + h * 64]
                        P.op('pe', lambda e, ps=ps, h=h, c=c, MT=MT: e.matmul(ps(), lhsT=MT[h][:], rhs=ST[c][:], start=True, stop=False), reads=[MT[h], ST[c]], writes=[bank[7]])
                        P.op('pe', lambda e, ps=ps, h=h, tmq=tmq, W=W: e.matmul(ps(), lhsT=tmq[:, 128 + h * 64:192 + h * 64], rhs=W[:, 64:128], start=False, stop=False),
                             reads=[tmq, W], writes=[bank[7]])
                        P.op('pe', lambda e, ps=ps, h=h, tmq=tmq: e.matmul(ps(), lhsT=tmq[:, 256 + h * 64:320 + h * 64], rhs=tmq[:, 0:64], start=False, stop=True),
                             reads=[tmq], writes=[bank[7]])
                        P.op('act', lambda e, ps=ps, c=c: e.copy(out=ST[c + 1][:], in_=ps()), reads=[bank[7]], writes=[ST[c + 1]])
                    if dbg_stage < 8:
                        continue
                    P.op('pe', lambda e, c_0=c_0, WATa=WATa: e.matmul(bank[7][:, 0:64], lhsT=WATa[:], rhs=ST[c_0][:], start=True, stop=False), reads=[WATa, ST[c_0]], writes=[bank[7]])
                    P.op('pe', lambda e, c_0=c_0, WATb=WATb: e.matmul(bank[7][:, 0:64], lhsT=WATb[:], rhs=ST[c_0 + 1][:], start=False, stop=True), reads=[WATb, ST[c_0 + 1]], writes=[bank[7]])
                    tt('dve', Ut[:], bank[7][:, 0:64], W[:, 64:128], ALU.add, [bank[7], W], [Ut])
                    P.op('pe', lambda e, c_0=c_0, cs=cs: e.matmul(bank[7][:, 64:128], lhsT=rhA[:, cs], rhs=ST[c_0][:], start=True, stop=False), reads=[rhA, ST[c_0]], writes=[bank[7]])
                    P.op('pe', lambda e, c_0=c_0, cs=cs: e.matmul(bank[7][:, 64:128], lhsT=rhB[:, cs], rhs=ST[c_0 + 1][:], start=False, stop=False), reads=[rhB, ST[c_0 + 1]], writes=[bank[7]])
                    P.op('pe', lambda e, Gq=Gq, Ut=Ut: e.matmul(bank[7][:, 64:128], lhsT=Gq[:, 128:256], rhs=Ut[:], start=False, stop=False), reads=[Gq, Ut], writes=[bank[7]])
                    P.op('pe', lambda e, Gq=Gq, tmq=tmq: e.matmul(bank[7][:, 64:128], lhsT=Gq[:, 384:512], rhs=tmq[:, 0:64], start=False, stop=True), reads=[Gq, tmq], writes=[bank[7]])
                    P.op('pe', lambda e, cs=cs: e.matmul(bank[7][:, 128:129], lhsT=rkr[:, cs], rhs=ones64[:, 0:1], start=True, stop=True), reads=[rkr, ones64], writes=[bank[7]])
                    if dbg_stage < 9:
                        continue
                    P.op('dve', lambda e: e.tensor_reduce(out=s1[:], in_=bank[7][:, 64:128], axis=AX.X, op=ALU.add), reads=[bank[7]], writes=[s1])
                    P.op('dve', lambda e: e.tensor_scalar(out=nmean[:], in0=s1[:], scalar1=-1.0 / 64, scalar2=None, op0=ALU.mult), reads=[s1], writes=[nmean])
                    P.op('dve', lambda e: e.tensor_scalar(out=ym[:], in0=bank[7][:, 64:128], scalar1=nmean[:, 0:1], scalar2=None, op0=ALU.add),
                         reads=[bank[7], nmean], writes=[ym])
                    P.op('act', lambda e: e.copy(out=pbs[:], in_=bank[7][:, 128:129]), reads=[bank[7]], writes=[pbs])
                    P.op('act', lambda e: e.activation(out=sqj[:], in_=ym[:], func=AF.Square, accum_out=s2[:]), reads=[ym], writes=[sqj, s2])
                    _rsqrt_mean(P, rstd, s2, 64, eps=RWKV_GN_EPS)
                    P.op('dve', lambda e: e.scalar_tensor_tensor(out=yo[:], in0=ym[:], scalar=rstd[:, 0:1], in1=lnx[:, 0, :], op0=ALU.mult, op1=ALU.mult),
                         reads=[ym, rstd, lnx], writes=[yo])
                    tt('pool', yo[:], yo[:], lnx[:, 1, :], ALU.add, [yo, lnx], [yo])
                    ot = ots[oi % 2]; oi += 1
                    P.op('dve', lambda e, ot=ot, tmq=tmq: e.scalar_tensor_tensor(out=ot[:], in0=tmq[:, 0:64], scalar=pbs[:, 0:1], in1=yo[:], op0=ALU.mult, op1=ALU.add),
                         reads=[tmq, pbs, yo], writes=[ot])
                    P.dma(d['o'][gti * 128:(gti + 1) * 128, :], ot[:], reads=[ot], q='pool')
                P.serial = False
            if j + 1 < njobs:
                pass
        P.finish()
    return nc


def rwkv_consts():
    c = hgrn_consts()
    j = np.arange(128)[:, None]; t = np.arange(128)[None, :]
    same = (j // 64 == t // 64)
    strictT = ((j < t) & same).astype(np.float32)
    inclT = ((j <= t) & same).astype(np.float32)
    mT4 = np.concatenate([strictT, inclT, strictT, inclT], 1)
    mstrict = np.ascontiguousarray(strictT.T)
    return dict(resetm=np.ascontiguousarray(c['resetm'][:, :RSEG]), cmA=np.ascontiguousarray(c['cmA'][:, :RSEG]),
                cmB=np.ascontiguousarray(c['cmB'][:, :RSEG]), mT4=mT4, mstrict=mstrict, ident=c['ident'],
                ones64=np.ones((64, 64), np.float32))


EVEN_SEGS = [('normrope', 0, 6), ('normrope', 384, 6), ('copy', 768, 6), ('silu', 1152, 6), ('copy', 1536, 19),
             ('silu', 2752, 6), ('norm', 3136, 4), ('silu', 3392, 4)]
ODD_SEGS = [('normrope', 0, 8), ('normrope', 512, 8), ('copy', 1024, 8), ('silu', 1536, 8), ('copy', 2048, 12),
            ('silu', 2816, 4), ('norm', 3072, 4), ('silu', 3328, 4)]
MEM_SEGS = [('norm', 0, 4), ('copy', 256, 4)]
EVEN_IN = 3648
ODD_IN = 3584

_CACHE = {}


def _prog(key, builder):
    if key not in _CACHE:
        _CACHE[key] = builder()
    return _CACHE[key]


def _rope_rep(pos):
    inv = (1.0 / (500000.0 ** (np.arange(0, 16, 2, dtype=np.float32) / np.float32(16)))).astype(np.float32)
    ang = (pos.astype(np.float32)[:, None] * inv[None, :]).astype(np.float32)
    c = np.cos(ang.astype(np.float64)).astype(np.float32)
    s = np.sin(ang.astype(np.float64)).astype(np.float32)
    return np.tile(c, (1, 8)), np.tile(s, (1, 8))


_IDENT = np.eye(128, dtype=np.float32)


def _run(nc, in_maps):
    res = run_bass_kernel_spmd(nc, in_maps, core_ids=list(range(NCORES)))
    return res.results


def _rwkv_job_inputs(bs, mu, w0, a0, k_k, k_a, r_k, w2, a2, lnx_g, lnx_b, b, h, pref):
    sl = lambda k: bs[b, :, k * 384 + h * 64:k * 384 + (h + 1) * 64].T
    x3 = np.zeros((3, 64, S + 1), np.float32)
    for k in range(3):
        x3[k, :, 1:] = sl(k)
    xl = np.zeros((2, 32, S + 1), np.float32)
    xl[0, :, 1:] = bs[b, :, 1152:1184].T
    xl[1, :, 1:] = bs[b, :, 1184:1216].T
    hs = slice(h * 64, (h + 1) * 64)
    pv = np.stack([mu[0:384][hs], mu[384:768][hs], mu[768:1152][hs], w0[hs], a0[hs], k_k[hs], k_a[hs], r_k[h]], 1)
    pl = np.stack([mu[1152:1184], mu[1184:1216]], 1)
    lnx = np.stack([np.tile(lnx_g[hs], (128, 1)), np.tile(lnx_b[hs], (128, 1))], 1)
    f = lambda a: np.ascontiguousarray(a, dtype=np.float32)
    return {pref + 'x3': x3, pref + 'xl': xl, pref + 'pv': f(pv), pref + 'pl': f(pl), pref + 'w2': f(w2[:, hs]),
            pref + 'a2': f(a2[:, hs]), pref + 'lnx': f(lnx)}


def _moba_consts():
    kk = np.arange(128)[:, None]
    qq = np.arange(512)[None, :]
    dm = [((r * 128 + kk) <= qq).astype(np.float32) for r in range(4)]
    ones = np.ones((128, 512), np.float32)
    zeros = np.zeros((128, 512), np.float32)
    masks_p = [np.ascontiguousarray(np.stack(dm + [zeros] * 4, 1)), np.ascontiguousarray(np.stack([ones] * 4 + dm, 1))]
    E = (np.arange(S)[None, :] // 256 == np.arange(32)[:, None]).astype(np.float32)
    pms = []
    for p in range(2):
        pm = np.zeros((32, 32), np.float32)
        j = np.arange(32)
        for i in range(8):
            for qt in range(4):
                own = ((2 * i + p) * 4 + qt) // 2
                pm[i * 4 + qt] = np.where(j < own, 0.0, np.where(j == own, 1e30, -1e30))
        pms.append(np.ascontiguousarray(np.tile(pm[None], (128, 1, 1))))
    qcols = [np.concatenate([np.arange((2 * i + p) * 512, (2 * i + p + 1) * 512) for i in range(8)]) for p in range(2)]
    dmask = np.ascontiguousarray(np.stack(dm, 1))
    return masks_p, E, pms, qcols, dmask


def kernel(x, mem, ln_g, mem_ln_g, w_mem_kv, m_qn_g, m_kn_g, e_w_in, e_w_out, a_qn_g, a_kn_g,
           b_mu, b_w0, b_w2, b_a0, b_a2, b_k_k, b_k_a, b_r_k, b_lnx_g, b_lnx_b,
           o_w_in, o_w_out, c_qn_g, c_kn_g, c_lq1, c_lk1, c_lq2, c_lk2, c_subln_g, d_lb, d_gn_g):
    import math
    f32 = lambda a: np.ascontiguousarray(np.asarray(a), dtype=np.float32)
    x = f32(x); mem = f32(mem)
    NT = B * S
    TPC = NT // NCORES
    xf = x.reshape(NT, D)
    masks_p, E, pms, qcols, dmask = _moba_consts()
    hcons = hgrn_consts()
    rcons = rwkv_consts()
    zc, zs = np.zeros((256, 64), np.float32), np.zeros((256, 64), np.float32)

    nc_mem = _prog('mem', lambda: build_proj(NMEM, 512, MEM_SEGS))
    in_maps = []
    for c in range(NCORES):
        b, li = c // 4, c % 4
        gains = np.zeros((128, 512), np.float32)
        gains[:, 0:256] = np.tile(f32(m_kn_g)[li], 4)
        in_maps.append(dict(x=mem[b], gx=np.tile(f32(mem_ln_g)[li], (128, 1)), w=f32(w_mem_kv)[li], gains=gains,
                            cosr=zc, sinr=zs, ident=_IDENT))
    r = _run(nc_mem, in_maps)
    kvm = {(c // 4, c % 4): r[c]['y'] for c in range(NCORES)}

    cos_sin = [_rope_rep(np.arange(TPC) + (c % 4) * TPC) for c in range(NCORES)]
    for li in range(DEPTH):
        j = li // 2
        even = (li % 2 == 0)
        nin = EVEN_IN if even else ODD_IN
        segs = EVEN_SEGS if even else ODD_SEGS
        nc_p = _prog(('proj', even), lambda: build_proj(TPC, nin, segs))
        gains = np.zeros((128, nin), np.float32)
        if even:
            gains[:, 0:384] = np.tile(f32(a_qn_g)[j], 6); gains[:, 384:768] = np.tile(f32(a_kn_g)[j], 6)
            gains[:, 3136:3392] = np.tile(f32(m_qn_g)[li], 4)
            w_in = f32(e_w_in)[j]; w_out = f32(e_w_out)[j]
        else:
            gains[:, 0:512] = np.tile(f32(c_qn_g)[j], 8); gains[:, 512:1024] = np.tile(f32(c_kn_g)[j], 8)
            gains[:, 3072:3328] = np.tile(f32(m_qn_g)[li], 4)
            w_in = f32(o_w_in)[j]; w_out = f32(o_w_out)[j]
        gx = np.tile(f32(ln_g)[li], (128, 1))
        in_maps = [dict(x=xf[c * TPC:(c + 1) * TPC], gx=gx, w=w_in, gains=gains, cosr=cos_sin[c][0], sinr=cos_sin[c][1],
                        ident=_IDENT) for c in range(NCORES)]
        r = _run(nc_p, in_maps)
        pr = np.concatenate([r[c]['y'] for c in range(NCORES)], 0).reshape(B, S, nin)
        T = lambda a: np.ascontiguousarray(a.T)
        o_all = np.zeros((B, S, D), np.float32)
        g_all = np.zeros((B, S, D), np.float32)
        if even:
            nc_a = _prog('att_even', lambda: build_att([dict(kind='moba')] * 3 + [dict(kind='mem')]))
            in_maps = []
            for c in range(NCORES):
                m = dict(ident=_IDENT)
                for s in range(3):
                    hj = 3 * c + s; p = hj % 2; bh = hj // 2; b, h = bh // 6, bh % 6
                    m['j%d_qT' % s] = T(pr[b, :, h * 64:(h + 1) * 64][qcols[p]])
                    m['j%d_kT' % s] = T(pr[b, :, 384 + h * 64:384 + (h + 1) * 64])
                    m['j%d_v' % s] = np.ascontiguousarray(pr[b, :, 768 + h * 64:768 + (h + 1) * 64])
                    m['j%d_masks' % s] = masks_p[p]; m['j%d_pm' % s] = pms[p]; m['j%d_E' % s] = E
                b, h = c // 4, c % 4
                kv = kvm[(b, li)]
                m['j3_qT'] = T(pr[b, :, 3136 + h * 64:3136 + (h + 1) * 64])
                m['j3_kT'] = T(kv[:, h * 64:(h + 1) * 64]); m['j3_v'] = np.ascontiguousarray(kv[:, 256 + h * 64:256 + (h + 1) * 64])
                in_maps.append(m)
            r = _run(nc_a, in_maps)
            for c in range(NCORES):
                for s in range(3):
                    hj = 3 * c + s; p = hj % 2; bh = hj // 2; b, h = bh // 6, bh % 6
                    o_all[b, qcols[p], h * 64:(h + 1) * 64] = r[c]['j%d_o' % s]
                b, h = c // 4, c % 4
                o_all[b, :, 768 + h * 64:768 + (h + 1) * 64] = r[c]['j3_o']
            nc_r = _prog('rwkv', lambda: build_rwkv(njobs=2))
            bs = pr[:, :, 1536:2752]
            in_maps = []
            for c in range(NCORES):
                m = dict(rcons)
                for s in range(2):
                    job = (2 * c + s) % 12; b, h = job // 6, job % 6
                    m.update(_rwkv_job_inputs(bs, f32(b_mu)[j], f32(b_w0)[j], f32(b_a0)[j], f32(b_k_k)[j], f32(b_k_a)[j], f32(b_r_k)[j],
                                              f32(b_w2)[j], f32(b_a2)[j], f32(b_lnx_g)[j], f32(b_lnx_b)[j], b, h, 'j%d_' % s))
                in_maps.append(m)
            r = _run(nc_r, in_maps)
            for c in range(6):
                for s in range(2):
                    job = 2 * c + s; b, h = job // 6, job % 6
                    o_all[b, :, 384 + h * 64:384 + (h + 1) * 64] = r[c]['j%d_o' % s]
            g_all[:, :, 0:384] = pr[:, :, 1152:1536]; g_all[:, :, 384:768] = pr[:, :, 2752:3136]; g_all[:, :, 768:1024] = pr[:, :, 3392:3648]
        else:
            lam_init = 0.8 - 0.6 * math.exp(-0.3 * li)
            nc_a = _prog(('att_odd', li), lambda: build_att([dict(kind='diff'), dict(kind='mem')], lam_init))
            lqk = np.stack([f32(c_lq1)[j], f32(c_lk1)[j], f32(c_lq2)[j], f32(c_lk2)[j]], 0)
            in_maps = []
            for c in range(NCORES):
                b, h = c // 4, c % 4
                m = dict(ident=_IDENT)
                for mp in range(2):
                    m['j0_qT%d' % mp] = T(pr[b, :, (2 * h + mp) * 64:(2 * h + mp + 1) * 64])
                    m['j0_kT%d' % mp] = T(pr[b, :, 512 + (2 * h + mp) * 64:512 + (2 * h + mp + 1) * 64])
                m['j0_v'] = np.ascontiguousarray(pr[b, :, 1024 + h * 128:1024 + (h + 1) * 128])
                m['j0_masks'] = dmask
                m['j0_lqk'] = np.ascontiguousarray(np.tile(lqk[None], (128, 1, 1))); m['j0_gsub'] = np.tile(f32(c_subln_g)[j], (128, 1))
                kv = kvm[(b, li)]
                m['j1_qT'] = T(pr[b, :, 3072 + h * 64:3072 + (h + 1) * 64])
                m['j1_kT'] = T(kv[:, h * 64:(h + 1) * 64]); m['j1_v'] = np.ascontiguousarray(kv[:, 256 + h * 64:256 + (h + 1) * 64])
                in_maps.append(m)
            r = _run(nc_a, in_maps)
            for c in range(NCORES):
                b, h = c // 4, c % 4
                o_all[b, :, h * 128:(h + 1) * 128] = r[c]['j0_o']
                o_all[b, :, 768 + h * 64:768 + (h + 1) * 64] = r[c]['j1_o']
            nc_h = _prog(('hgrn', j), lambda: build_hgrn(1 if j >= 1 else 0))
            in_maps = []
            for c in range(NCORES):
                b, h = c // 4, c % 4
                m = dict(hcons)
                m['qT'] = T(pr[b, :, 2048 + h * 64:2048 + (h + 1) * 64]); m['fT'] = T(pr[b, :, 2304 + h * 64:2304 + (h + 1) * 64])
                m['v'] = np.ascontiguousarray(pr[b, :, 2560 + h * 64:2560 + (h + 1) * 64])
                m['dlb'] = T(f32(d_lb)[0:2, h * 64:(h + 1) * 64]); m['gng'] = np.tile(f32(d_gn_g)[j], (128, 1))
                in_maps.append(m)
            r = _run(nc_h, in_maps)
            for c in range(NCORES):
                b, h = c // 4, c % 4
                o_all[b, :, 512 + h * 64:512 + (h + 1) * 64] = r[c]['o']
            g_all[:, :, 0:512] = pr[:, :, 1536:2048]; g_all[:, :, 512:768] = pr[:, :, 2816:3072]; g_all[:, :, 768:1024] = pr[:, :, 3328:3584]
        nc_o = _prog('out', lambda: build_out(TPC))
        of = o_all.reshape(NT, D); gf = g_all.reshape(NT, D)
        in_maps = [dict(x=xf[c * TPC:(c + 1) * TPC], oT=T(of[c * TPC:(c + 1) * TPC]), gT=T(gf[c * TPC:(c + 1) * TPC]), w=w_out)
                   for c in range(NCORES)]
        r = _run(nc_o, in_maps)
        xf = np.concatenate([r[c]['y'] for c in range(NCORES)], 0)
    return xf.reshape(B, S, D).astype(np.float32)
```
